# Optimizing a Trainium2 kernel written in Bass

```python
import math
import jax, jax.numpy as jnp
from jax import lax
import numpy as np

D_MODEL = 1024
BATCH = 8
SEQ = 8192
DEPTH = 1

N_HEADS = 8
HEAD_DIM = 64
V_HEAD_DIM = 2 * HEAD_DIM
ATTN_WIDTH = N_HEADS * V_HEAD_DIM
QK_WIDTH = N_HEADS * 2 * HEAD_DIM
CONV_WIDTH = D_MODEL
SHORT_CONV_K = 3
FFN_CONV_K = 3
D_FF = 2816
ROPE_THETA = 10000.0
Q_BLOCK = 128
NORM_EPS = 1e-6
SPLIT_SIZES = (QK_WIDTH, QK_WIDTH, ATTN_WIDTH, CONV_WIDTH, CONV_WIDTH, CONV_WIDTH, D_MODEL, D_MODEL)
SPLIT_POINTS = tuple(int(s) for s in np.cumsum(SPLIT_SIZES)[:-1])
D_IN = int(sum(SPLIT_SIZES))

kernel_name = 'hybrid_diffattn_shortconv_gated_block'


def rms_norm(x, g):
    xf = x.astype(jnp.float32)
    y = xf * lax.rsqrt(jnp.mean(xf * xf, axis=-1, keepdims=True) + NORM_EPS)
    return (y * g.astype(jnp.float32)).astype(x.dtype)


def causal_dwconv(x, w, b=None):
    k_width, chans = w.shape
    y = lax.conv_general_dilated(
        x, w[:, None, :].astype(x.dtype), window_strides=(1,),
        padding=((k_width - 1, 0),), dimension_numbers=('NWC', 'WIO', 'NWC'),
        feature_group_count=chans)
    if b is not None:
        y = y + b.astype(x.dtype)
    return y


def rope_tables(seq, dim):
    inv = ROPE_THETA ** (-jnp.arange(0, dim, 2, dtype=jnp.float32) / dim)
    ang = jnp.arange(seq, dtype=jnp.float32)[:, None] * inv[None, :]
    return jnp.cos(ang), jnp.sin(ang)


def apply_rope(x, cos, sin):
    half = x.shape[-1] // 2
    c = cos[:, None, None, :].astype(x.dtype)
    s = sin[:, None, None, :].astype(x.dtype)
    x1, x2 = x[..., :half], x[..., half:]
    return jnp.concatenate([x1 * c - x2 * s, x2 * c + x1 * s], axis=-1)


def diff_attention(q, k, v, lam):
    bsz, seq = q.shape[0], q.shape[1]
    n_blk = seq // Q_BLOCK
    scale = HEAD_DIM ** -0.5
    qb = q.reshape(bsz, n_blk, Q_BLOCK, N_HEADS, 2, HEAD_DIM).swapaxes(0, 1)
    starts = jnp.arange(n_blk, dtype=jnp.int32) * Q_BLOCK
    kpos = jnp.arange(seq, dtype=jnp.int32)

    def one_block(args):
        q_blk, s0 = args
        s = jnp.einsum('bqhmd,bkhmd->bhmqk', q_blk, k).astype(jnp.float32) * scale
        qpos = s0 + jnp.arange(Q_BLOCK, dtype=jnp.int32)
        causal = kpos[None, :] <= qpos[:, None]
        s = jnp.where(causal, s, -jnp.inf)
        p = jax.nn.softmax(s, axis=-1)
        a = p[:, :, 0] - lam.astype(jnp.float32) * p[:, :, 1]
        return jnp.einsum('bhqk,bkhe->bqhe', a.astype(v.dtype), v)

    out = lax.map(one_block, (qb, starts))
    return out.swapaxes(0, 1).reshape(bsz, seq, N_HEADS, V_HEAD_DIM)


def setup_inputs(seed: int = 0) -> dict:
    key = jax.random.key(seed)
    ks = jax.random.split(key, 20)
    f32 = jnp.float32
    nrm = lambda k, shape, s: jax.random.normal(k, shape, f32) * s
    return {
        'x': nrm(ks[0], (BATCH, SEQ, D_MODEL), 1.0),
        'attn_norm_g': 1.0 + nrm(ks[1], (DEPTH, D_MODEL), 0.02),
        'w_in': nrm(ks[2], (DEPTH, D_MODEL, D_IN), D_MODEL ** -0.5),
        'b_gate': nrm(ks[3], (DEPTH, 2 * D_MODEL), 0.02),
        'q_norm_g': 1.0 + nrm(ks[4], (DEPTH, HEAD_DIM), 0.02),
        'k_norm_g': 1.0 + nrm(ks[5], (DEPTH, HEAD_DIM), 0.02),
        'lambda_q1': nrm(ks[6], (DEPTH, HEAD_DIM), 0.1),
        'lambda_k1': nrm(ks[7], (DEPTH, HEAD_DIM), 0.1),
        'lambda_q2': nrm(ks[8], (DEPTH, HEAD_DIM), 0.1),
        'lambda_k2': nrm(ks[9], (DEPTH, HEAD_DIM), 0.1),
        'subln_g': 1.0 + nrm(ks[10], (DEPTH, V_HEAD_DIM), 0.02),
        'short_conv_w': nrm(ks[11], (DEPTH, SHORT_CONV_K, CONV_WIDTH), SHORT_CONV_K ** -0.5),
        'w_out': nrm(ks[12], (DEPTH, D_MODEL, D_MODEL), D_MODEL ** -0.5),
        'ffn_norm_g': 1.0 + nrm(ks[13], (DEPTH, D_MODEL), 0.02),
        'w_up': nrm(ks[14], (DEPTH, D_MODEL, 2 * D_FF), D_MODEL ** -0.5),
        'ffn_conv_w': nrm(ks[15], (DEPTH, FFN_CONV_K, 2 * D_FF), FFN_CONV_K ** -0.5),
        'ffn_conv_b': nrm(ks[16], (DEPTH, 2 * D_FF), 0.02),
        'w_down': nrm(ks[17], (DEPTH, D_FF, D_MODEL), D_FF ** -0.5),
    }


def reference(x, attn_norm_g, w_in, b_gate, q_norm_g, k_norm_g, lambda_q1, lambda_k1,
              lambda_q2, lambda_k2, subln_g, short_conv_w, w_out, ffn_norm_g, w_up,
              ffn_conv_w, ffn_conv_b, w_down):
    bsz, seq = x.shape[0], x.shape[1]
    cos, sin = rope_tables(seq, HEAD_DIM)
    for l in range(DEPTH):
        lambda_init = 0.8 - 0.6 * math.exp(-0.3 * l)
        h = rms_norm(x, attn_norm_g[l])
        proj = h @ w_in[l]
        q, k, v, c_b, c_c, c_x, g_a, g_c = jnp.split(proj, SPLIT_POINTS, axis=-1)
        q = q.reshape(bsz, seq, N_HEADS, 2, HEAD_DIM)
        k = k.reshape(bsz, seq, N_HEADS, 2, HEAD_DIM)
        v = v.reshape(bsz, seq, N_HEADS, V_HEAD_DIM)
        q = apply_rope(rms_norm(q, q_norm_g[l]), cos, sin)
        k = apply_rope(rms_norm(k, k_norm_g[l]), cos, sin)
        lam = (jnp.exp(jnp.sum(lambda_q1[l].astype(jnp.float32) * lambda_k1[l].astype(jnp.float32)))
               - jnp.exp(jnp.sum(lambda_q2[l].astype(jnp.float32) * lambda_k2[l].astype(jnp.float32)))
               + lambda_init)
        o = diff_attention(q, k, v, lam)
        o = rms_norm(o, subln_g[l]) * (1.0 - lambda_init)
        y_attn = o.reshape(bsz, seq, ATTN_WIDTH)
        y_conv = c_b * causal_dwconv(c_c * c_x, short_conv_w[l])
        gate_a = jax.nn.sigmoid(g_a + b_gate[l, :D_MODEL])
        gate_c = jax.nn.sigmoid(g_c + b_gate[l, D_MODEL:])
        merged = gate_a * y_attn + gate_c * y_conv
        x = x + merged @ w_out[l]
        h = rms_norm(x, ffn_norm_g[l])
        u = causal_dwconv(h @ w_up[l], ffn_conv_w[l], ffn_conv_b[l])
        a, b = jnp.split(u, 2, axis=-1)
        x = x + (jax.nn.silu(a) * b) @ w_down[l]
    return x
```

```python
import math
from contextlib import ExitStack

import numpy as np
import concourse.bass as bass
import concourse.mybir as mybir
from concourse.bass_utils import run_bass_kernel_spmd

F32 = mybir.dt.float32
BF16 = mybir.dt.bfloat16
AF = mybir.ActivationFunctionType
ALU = mybir.AluOpType
AX = mybir.AxisListType

D = 1024
NH = 8
DFF = 2816
DIN = 8192
EPS = 1e-6
LAMBDA_INIT = 0.8 - 0.6 * math.exp(0.0)
NCORES = 8


class Ev:
    def __init__(self, nc, es, name, step=1):
        self.sem = es.enter_context(nc.semaphore(name))
        self.n = 0
        self.step = step


class Op:
    __slots__ = ("fn", "waits", "ev", "val", "stream", "hard")

    def __init__(self, fn, waits, ev, stream, hard=()):
        self.fn = fn
        self.hard = [w for w in hard if w is not None]
        self.waits = [w for w in waits if w is not None] + self.hard
        self.ev = ev
        self.val = None
        self.stream = stream


class Stream:
    def __init__(self, name, ev=None):
        self.name = name
        self.ops = []
        self.ev = ev

    def add(self, fn, waits=(), ev="default", sig=True, hard=()):
        if ev == "default":
            ev = self.ev if sig else None
        op = Op(fn, list(waits), ev, self, hard)
        self.ops.append(op)
        return op


def assign_values(streams):
    for s in streams:
        for op in s.ops:
            if op.ev is not None:
                op.ev.n += op.ev.step
                op.val = op.ev.n


def emit_stream(stream, eng):
    waited = {}
    for op in stream.ops:
        need = {}
        for w in op.waits:
            if w.ev is None:
                raise RuntimeError("waiting on op without event")
            if w.stream is stream and w.ev is stream.ev and not any(w is h for h in op.hard):
                continue
            if need.get(w.ev, 0) < w.val:
                need[w.ev] = w.val
        for ev, val in need.items():
            if waited.get(ev, 0) < val:
                eng.wait_ge(ev.sem, val)
                waited[ev] = val
        ins = op.fn(eng)
        if op.ev is not None:
            ins.then_inc(op.ev.sem, op.ev.step)


def run_block(nc, streams):
    assign_values(list(streams.values()))
    with nc.Block() as block:
        if "pe" in streams:
            @block.tensor
            def _(e):
                emit_stream(streams["pe"], e)
        if "act" in streams:
            @block.scalar
            def _(e):
                emit_stream(streams["act"], e)
        if "dve" in streams:
            @block.vector
            def _(e):
                emit_stream(streams["dve"], e)
        if "pool" in streams:
            @block.gpsimd
            def _(e):
                emit_stream(streams["pool"], e)
        if "sp" in streams:
            @block.sync
            def _(e):
                emit_stream(streams["sp"], e)


def mk_streams(nc, es, tag):
    st = {}
    for nm in ("pe", "act", "dve", "pool"):
        st[nm] = Stream(nm, Ev(nc, es, f"{tag}_{nm}"))
    st["sp"] = Stream("sp", None)
    return st


def build_program(S, debug=False, phases="ABC"):
    NT = S // 512
    NCH = S // 128
    nc = bass.Bass("TRN2", target_bir_lowering=False)
    es = ExitStack()

    def din(name, shape, dt=F32):
        return nc.dram_tensor(name, list(shape), dt, kind="ExternalInput").ap()

    okind = "ExternalOutput" if debug else "Internal"

    def dscr(name, shape, dt=BF16):
        return nc.dram_tensor(name, list(shape), dt, kind=okind).ap()

    x = din("x", [S, D])
    w_in = din("w_in", [D, DIN])
    w_out = din("w_out", [D, D])
    w_up = din("w_up", [D, 2 * DFF])
    w_dn = din("w_dn", [DFF, D])
    g1bc_d = din("g1bc", [128, D])
    g2bc_d = din("g2bc", [128, D])
    qg_d = din("qg", [128, 64])
    kg_d = din("kg", [128, 64])
    lam_d = din("lam4", [128, 4, 64])
    sub_d = din("subg", [128, 128])
    bg_d = din("bg", [128, 16])
    scw_d = din("scw", [128, 3, 8])
    fcw_d = din("fcw", [128, 3, 44])
    fcb_d = din("fcb", [128, 44])
    cos_d = din("cosT", [128, NCH, 32])
    sin_d = din("sinT", [128, NCH, 32])
    ident_d = din("ident", [128, 128])
    tri_d = din("tri", [128, 128])
    mneg_d = din("mneg", [128, 128])
    out = nc.dram_tensor("out", [S, D], F32, kind="ExternalOutput").ap()

    Wb_in = dscr("Wb_in", [D, DIN]) if not debug else nc.dram_tensor("Wb_in", [D, DIN], BF16, kind="Internal").ap()
    Wb_out = nc.dram_tensor("Wb_out", [D, D], BF16, kind="Internal").ap()
    Wb_up = nc.dram_tensor("Wb_up", [D, 2 * DFF], BF16, kind="Internal").ap()
    Wb_dn = nc.dram_tensor("Wb_dn", [DFF, D], BF16, kind="Internal").ap()
    QT = dscr("QT", [NH, 128, S])
    KT = dscr("KT", [NH, 128, S])
    Vs = dscr("Vs", [NH, 128, NCH, 128])
    GA = dscr("GA", [NH, 128, S])
    YC = dscr("YC", [NH, 128, S])
    MT = dscr("MT", [NH, 128, S])

    pall = es.enter_context(nc.psum_tensor("pall", [128, 8, 512], F32))

    def bank_bf(b):
        return pall[:, b, :].bitcast(BF16)

    ev_wrest = Ev(nc, es, "wrest", 16)

    def phase_A():
        with ExitStack() as ps:
            def sb(name, shape, dt=F32):
                return ps.enter_context(nc.sbuf_tensor("A_" + name, list(shape), dt))

            st = mk_streams(nc, es, "A")
            PE, ACT, DVE, POOL, SP = st["pe"], st["act"], st["dve"], st["pool"], st["sp"]
            ev_const = Ev(nc, es, "A_const", 16)
            ev_win = Ev(nc, es, "A_win", 16)
            ev_ldx = [Ev(nc, es, f"A_ldx{i}", 16) for i in range(2)]
            ev_ldw = [Ev(nc, es, f"A_ldw{i}", 16) for i in range(3)]
            ev_stqk = [Ev(nc, es, f"A_stqk{i}", 16) for i in range(2)]
            ev_stv = [Ev(nc, es, f"A_stv{i}", 16) for i in range(8)]
            ev_styc = Ev(nc, es, "A_styc", 16)
            ev_stga = Ev(nc, es, "A_stga", 16)

            xin = [sb(f"xin{i}", [128, 4, D]) for i in range(2)]
            hb = sb("hb", [128, 4, D], BF16)
            hT = [sb(f"hT{i}", [128, 8, 512], BF16) for i in range(2)]
            Wb = [sb(f"W{i}", [128, 8, 512], BF16) for i in range(3)]
            cosT = sb("cos", [128, NCH, 32])
            sinT = sb("sin", [128, NCH, 32])
            g1bc = sb("g1bc", [128, D])
            qg = sb("qg", [128, 64])
            kg = sb("kg", [128, 64])
            neghalf = sb("neghalf", [128, 8])
            ss4 = sb("ss4", [128, 4])
            var4 = sb("var4", [128, 4])
            rstd4 = sb("rstd4", [128, 4])
            junk = sb("junk", [128, D])
            sq = [sb(f"sq{i}", [128, 512]) for i in range(2)]
            t0 = [sb(f"t0{i}", [128, 512]) for i in range(2)]
            tA = [sb(f"tA{i}", [128, 512]) for i in range(2)]
            tB = [sb(f"tB{i}", [128, 512]) for i in range(2)]
            oo = [sb(f"oo{i}", [128, 512]) for i in range(2)]
            ssq = [sb(f"ssq{i}", [128, 8]) for i in range(2)]
            var8 = [sb(f"var8{i}", [128, 8]) for i in range(2)]
            rs8 = [sb(f"rs8{i}", [128, 8]) for i in range(2)]
            qn = [sb(f"qn{i}", [128, 512], BF16) for i in range(8)]
            QTt = [sb(f"QTt{i}", [128, 4, 512], BF16) for i in range(2)]
            vt = [sb(f"vt{i}", [128, 512], BF16) for i in range(8)]
            ubuf = sb("ubuf", [128, 8, 514])
            ycv = sb("ycv", [128, 8, 512])
            sg = [sb(f"sg{i}", [128, 512]) for i in range(2)]
            YCt = sb("YCt", [128, 8, 512], BF16)
            GAt = sb("GAt", [128, 8, 512], BF16)
            bg = sb("bg", [128, 16])
            scw = sb("scw", [128, 3, 8])
            ident = sb("ident", [128, 128], BF16)

            GROUPS = [("q", 0, 0), ("v", 2048, 0), ("q", 512, 1), ("cc", 4096, 0), ("k", 1024, 0), ("v", 2560, 1),
                      ("cx", 5120, 0), ("cc", 4608, 1), ("k", 1536, 1), ("ga", 6144, 0), ("cx", 5632, 1), ("ga", 6656, 1),
                      ("cb", 3072, 0), ("gc", 7168, 0), ("cb", 3584, 1), ("gc", 7680, 1)]
            consts = [(cosT, cos_d), (sinT, sin_d), (g1bc, g1bc_d), (qg, qg_d), (kg, kg_d),
                      (bg, bg_d), (scw, scw_d)]
            c_ops = []
            for dst, src in consts:
                c_ops.append(SP.add(lambda e, dst=dst, src=src: e.dma_start(out=dst[:], in_=src), ev=ev_const))
            c_last = c_ops[-1]
            id_op = POOL.add(lambda e: e.dma_start(out=ident[:], in_=ident_d), ev=ev_win)
            win_last = id_op
            ev_wc = [Ev(nc, es, f"A_wc{i}", 16) for i in range(16)]
            win_ops = {}

            def pool_cast(gi_):
                col_ = GROUPS[gi_][1]
                win_ops[gi_] = POOL.add(
                    lambda e: e.dma_start(out=Wb_in[:, col_:col_ + 512], in_=w_in[:, col_:col_ + 512]), ev=ev_wc[gi_])
            for gi_ in range(3):
                pool_cast(gi_)
            m0 = DVE.add(lambda e: e.memset(neghalf[:], -0.5))
            m1 = DVE.add(lambda e: e.memset(ubuf[:], 0.0))

            x_v = x.rearrange("(t s p) d -> t p s d", s=4, p=128)
            Wv = Wb_in.rearrange("(kc p) n -> p kc n", p=128)

            NG = len(GROUPS)

            state = dict(xload={}, prep_hb=None, rd={}, pe_unit={}, wload={}, last_unit_of_group={},
                         hT_ready={}, trh_evac=[], trq_evac=None, pe_trq={}, qk_cnt=0, v_cnt=0,
                         dve_red={}, dve_fin={}, st_qk=[None, None], st_v=[None] * 8, st_yc=None, st_ga=None,
                         halo=[m1] * 8, pool_yc=[None] * 8, sg_pool=[None, None], gc_cnt=0,
                         act_sq4={}, dve_hb_last={}, pe_trh_last={}, pending_stores=[])

            def sp_xload(t):
                slot = t % 2
                waits = []
                if t >= 2:
                    waits += [state["dve_hb_last"][t - 2], state["act_sq4"][t - 2]]
                state["xload"][t] = SP.add(lambda e: e.dma_start(out=xin[slot][:], in_=x_v[t]), waits, ev=ev_ldx[slot])

            def act_prep(t):
                slot = t % 2
                op = None
                for s in range(4):
                    op = ACT.add(lambda e, s=s: e.activation(out=junk[:], in_=xin[slot][:, s, :], func=AF.Square,
                                                            accum_out=ss4[:, s:s + 1]),
                                 [state["xload"][t]])
                state["act_sq4"][t] = op

            def dve_prep(t):
                slot = t % 2
                dv = DVE.add(lambda e: e.tensor_scalar(out=var4[:], in0=ss4[:], scalar1=1.0 / D, scalar2=EPS,
                                                       op0=ALU.mult, op1=ALU.add), [state["act_sq4"][t]])
                pw = POOL.add(lambda e: e.tensor_tensor(out=rstd4[:], in0=var4[:], in1=neghalf[:, 0:4], op=ALU.pow),
                              [dv, m0])
                ops = []
                for s in range(4):
                    w = [pw, c_last]
                    if t >= 1:
                        w.append(state["pe_trh_last"][t - 1])
                    ops.append(DVE.add(lambda e, s=s: e.scalar_tensor_tensor(
                        out=hb[:, s, :], in0=xin[slot][:, s, :], scalar=rstd4[:, s:s + 1], in1=g1bc[:],
                        op0=ALU.mult, op1=ALU.mult), w))
                state["hb_ops", t] = ops
                state["dve_hb_last"][t] = ops[-1]

            def pe_trh(t, s):
                w = [state["hb_ops", t][s], win_last]
                ev_list = state["trh_evac"]
                if len(ev_list) >= 2:
                    w.append(ev_list[-2])
                bk = 5 + (s % 2)

                def fn(e):
                    ins = None
                    for kc in range(8):
                        ins = e.transpose(out=bank_bf(bk)[:, kc * 128:(kc + 1) * 128],
                                          in_=hb[:, s, kc * 128:(kc + 1) * 128], identity=ident[:])
                    return ins
                op = PE.add(fn, w)
                state["pe_trh", t, s] = op
                if s == 3:
                    state["pe_trh_last"][t] = op

            def act_trh_evac(t, s):
                bk = 5 + (s % 2)
                op = ACT.add(lambda e: e.copy(out=hT[t % 2][:, :, s * 128:(s + 1) * 128],
                                              in_=bank_bf(bk).rearrange("p (k t) -> p k t", k=8)),
                             [state["pe_trh", t, s]])
                state["trh_evac"].append(op)
                if s == 3:
                    state["hT_ready"][t] = op

            def sp_wload(t, gi):
                gg = t * NG + gi
                slot = gg % 3
                col = GROUPS[gi][1]
                waits = [win_ops[gi]]
                if gg >= 3:
                    waits.append(state["last_unit_of_group"][gg - 3])
                state["wload"][gg] = SP.add(lambda e: e.dma_start(out=Wb[slot][:], in_=Wv[:, :, col:col + 512]),
                                            waits, ev=ev_ldw[slot])

            def pe_unit(t, gi, s):
                gg = t * NG + gi
                gu = gg * 4 + s
                kind = GROUPS[gi][0]
                slot = gg % 3
                bk = gu % 5
                waits = [state["wload"][gg], state["hT_ready"][t]]
                if gu >= 5:
                    waits += state["rd"][gu - 5]
                tok_major = kind in ("q", "k", "v")

                def fn(e):
                    ins = None
                    for kc in range(8):
                        if tok_major:
                            lhsT = hT[t % 2][:, kc, s * 128:(s + 1) * 128]
                            rhs = Wb[slot][:, kc, :]
                        else:
                            lhsT = Wb[slot][:, kc, s * 128:(s + 1) * 128]
                            rhs = hT[t % 2][:, kc, :]
                        ins = e.matmul(pall[:, bk, :], lhsT=lhsT, rhs=rhs, start=(kc == 0), stop=(kc == 7))
                    return ins
                op = PE.add(fn, waits)
                state["pe_unit"][gu] = op
                if s == 3:
                    state["last_unit_of_group"][gg] = op
                return op

            def consume_qk(t, gi, s, mm):
                gu = (t * NG + gi) * 4 + s
                bk = gu % 5
                n = state["qk_cnt"]
                state["qk_cnt"] += 1
                wbi = n % 2
                gsel = qg if GROUPS[gi][0] == "q" else kg
                cidx = t * 4 + s
                w = [mm]
                if n >= 2:
                    w.append(state["dve_red"][n - 2])
                a1 = ACT.add(lambda e: e.activation(out=sq[wbi][:], in_=pall[:, bk, :], func=AF.Square), w)
                red = DVE.add(lambda e: e.tensor_reduce(out=ssq[wbi][:], in_=sq[wbi][:].rearrange("p (g d) -> p g d", g=8),
                                                        axis=AX.X, op=ALU.add), [a1])
                state["dve_red"][n] = red
                dv = DVE.add(lambda e: e.tensor_scalar(out=var8[wbi][:], in0=ssq[wbi][:], scalar1=1.0 / 64, scalar2=EPS,
                                                       op0=ALU.mult, op1=ALU.add), hard=[red])
                pw = POOL.add(lambda e: e.tensor_tensor(out=rs8[wbi][:], in0=var8[wbi][:], in1=neghalf[:], op=ALU.pow),
                              [dv, m0])
                v3 = lambda ap: ap.rearrange("p (g d) -> p g d", g=8)
                v4 = lambda ap: ap.rearrange("p (g h d) -> p g h d", g=8, h=2)
                d1 = DVE.add(lambda e: e.tensor_tensor(out=v3(t0[wbi][:]), in0=v3(pall[:, bk, :]),
                                                       in1=gsel[:].unsqueeze(1).broadcast_to([128, 8, 64]), op=ALU.mult),
                             [mm, c_last, a1])
                state["rd"][gu] = [a1, d1]
                cb4 = cosT[:, cidx, :].unsqueeze(1).unsqueeze(1).broadcast_to([128, 8, 2, 32])
                sb4 = sinT[:, cidx, :].unsqueeze(1).unsqueeze(1).broadcast_to([128, 8, 2, 32])
                DVE.add(lambda e: e.tensor_tensor(out=v4(tA[wbi][:]), in0=v4(t0[wbi][:]), in1=cb4, op=ALU.mult), sig=False)
                DVE.add(lambda e: e.tensor_tensor(out=v4(tB[wbi][:]), in0=v4(t0[wbi][:]), in1=sb4, op=ALU.mult), sig=False)
                DVE.add(lambda e: e.tensor_tensor(out=v4(oo[wbi][:])[:, :, 0, :], in0=v4(tA[wbi][:])[:, :, 0, :],
                                                  in1=v4(tB[wbi][:])[:, :, 1, :], op=ALU.subtract), sig=False)
                DVE.add(lambda e: e.tensor_tensor(out=v4(oo[wbi][:])[:, :, 1, :], in0=v4(tA[wbi][:])[:, :, 1, :],
                                                  in1=v4(tB[wbi][:])[:, :, 0, :], op=ALU.add), sig=False)
                w = [pw]
                if n >= 8:
                    w.append(state["pe_trq"][n - 8])
                fin = DVE.add(lambda e: e.tensor_tensor(out=v3(qn[n % 8][:]), in0=v3(oo[wbi][:]),
                                                        in1=rs8[wbi][:].unsqueeze(2).broadcast_to([128, 8, 64]),
                                                        op=ALU.mult), w)
                state["dve_fin"][n] = fin
                return n

            def pe_trq(n):
                wbi = n % 8
                w = [state["dve_fin"][n]]
                if state["trq_evac"] is not None:
                    w.append(state["trq_evac"])

                def fn(e):
                    ins = None
                    for hh in range(4):
                        ins = e.transpose(out=bank_bf(7)[:, hh * 128:(hh + 1) * 128],
                                          in_=qn[wbi][:, hh * 128:(hh + 1) * 128], identity=ident[:])
                    return ins
                state["pe_trq"][n] = PE.add(fn, w)

            def act_trq_evac(n, t, gi, s):
                slot = state["qkg", t, gi] % 2
                w = [state["pe_trq"][n]]
                if s == 0 and state["st_qk"][slot] is not None:
                    w.append(state["st_qk"][slot])
                op = ACT.add(lambda e: e.copy(out=QTt[slot][:, :, s * 128:(s + 1) * 128],
                                              in_=bank_bf(7)[:, 0:512].rearrange("p (h t) -> p h t", h=4)), w)
                state["trq_evac"] = op
                if s == 3:
                    dst = QT if GROUPS[gi][0] == "q" else KT
                    h0 = GROUPS[gi][2] * 4
                    state["pending_stores"].append((t * NG + gi, lambda: _store_qk(slot, dst, h0, t, op)))

            def _store_qk(slot, dst, h0, t, op):
                state["st_qk"][slot] = SP.add(
                    lambda e: e.dma_start(out=dst[h0:h0 + 4, :, t * 512:(t + 1) * 512].rearrange("h p t -> p h t"),
                                          in_=QTt[slot][:]), [op], ev=ev_stqk[slot])

            def consume_v(t, gi, s, mm):
                gu = (t * NG + gi) * 4 + s
                bk = gu % 5
                n = state["v_cnt"]
                state["v_cnt"] += 1
                slot = n % 8
                w = [mm]
                if state["st_v"][slot] is not None:
                    w.append(state["st_v"][slot])
                a = ACT.add(lambda e: e.copy(out=vt[slot][:], in_=pall[:, bk, :]), w)
                state["rd"][gu] = [a]
                h0 = GROUPS[gi][2] * 4
                c = t * 4 + s

                def mk():
                    state["st_v"][slot] = SP.add(
                        lambda e: e.dma_start(out=Vs[h0:h0 + 4, :, c, :].rearrange("h p e -> p h e"),
                                              in_=vt[slot][:].rearrange("p (h e) -> p h e", h=4)), [a], ev=ev_stv[slot])
                state["pending_stores"].append((t * NG + gi, mk))

            def consume_conv(t, gi, s, mm):
                gu = (t * NG + gi) * 4 + s
                bk = gu % 5
                kind = GROUPS[gi][0]
                c = GROUPS[gi][2] * 4 + s
                if kind == "cc":
                    a = ACT.add(lambda e: e.copy(out=ubuf[:, c, 2:514], in_=pall[:, bk, :]), [mm, state["halo"][c]])
                    state["cc", c] = a
                    state["rd"][gu] = [a]
                elif kind == "cx":
                    d = DVE.add(lambda e: e.tensor_tensor(out=ubuf[:, c, 2:514], in0=pall[:, bk, :], in1=ubuf[:, c, 2:514],
                                                          op=ALU.mult), [mm, state["cc", c]])
                    state["rd"][gu] = [d]
                    w = [c_last]
                    if state["pool_yc"][c] is not None:
                        w.append(state["pool_yc"][c])
                    DVE.add(lambda e: e.tensor_scalar(out=ycv[:, c, :], in0=ubuf[:, c, 2:514], scalar1=scw[:, 2, c:c + 1],
                                                      scalar2=None, op0=ALU.mult), w, sig=False)
                    DVE.add(lambda e: e.scalar_tensor_tensor(out=ycv[:, c, :], in0=ubuf[:, c, 1:513], scalar=scw[:, 1, c:c + 1],
                                                             in1=ycv[:, c, :], op0=ALU.mult, op1=ALU.add), sig=False)
                    DVE.add(lambda e: e.scalar_tensor_tensor(out=ycv[:, c, :], in0=ubuf[:, c, 0:512], scalar=scw[:, 0, c:c + 1],
                                                             in1=ycv[:, c, :], op0=ALU.mult, op1=ALU.add), sig=False)
                    state["halo"][c] = DVE.add(lambda e: e.tensor_copy(out=ubuf[:, c, 0:2], in_=ubuf[:, c, 512:514]))
                elif kind == "cb":
                    d = DVE.add(lambda e: e.tensor_tensor(out=ycv[:, c, :], in0=pall[:, bk, :], in1=ycv[:, c, :],
                                                          op=ALU.mult), [mm])
                    state["cb", c] = d
                    state["rd"][gu] = [d]
                elif kind == "gc":
                    n = state["gc_cnt"]
                    state["gc_cnt"] += 1
                    si = n % 2
                    w = [mm, c_last]
                    if state["sg_pool"][si] is not None:
                        w.append(state["sg_pool"][si])
                    a = ACT.add(lambda e: e.activation(out=sg[si][:], in_=pall[:, bk, :], func=AF.Sigmoid,
                                                       bias=bg[:, 8 + c:9 + c]), w)
                    state["rd"][gu] = [a]
                    w = [a, state["cb", c]]
                    if c == 0 and state["st_yc"] is not None:
                        w.append(state["st_yc"])
                    p = POOL.add(lambda e: e.tensor_tensor(out=YCt[:, c, :], in0=sg[si][:], in1=ycv[:, c, :], op=ALU.mult), w)
                    state["sg_pool"][si] = p
                    state["pool_yc"][c] = p
                    if c == 7:
                        def mk():
                            state["st_yc"] = SP.add(
                                lambda e: e.dma_start(out=YC[:, :, t * 512:(t + 1) * 512].rearrange("c p t -> p c t"),
                                                      in_=YCt[:]), [p], ev=ev_styc)
                        state["pending_stores"].append((t * NG + gi, mk))
                elif kind == "ga":
                    w = [mm, c_last]
                    if c == 0 and state["st_ga"] is not None:
                        w.append(state["st_ga"])
                    a = ACT.add(lambda e: e.activation(out=GAt[:, c, :], in_=pall[:, bk, :], func=AF.Sigmoid,
                                                       bias=bg[:, c:c + 1]), w)
                    state["rd"][gu] = [a]
                    if c == 7:
                        def mk():
                            state["st_ga"] = SP.add(
                                lambda e: e.dma_start(out=GA[:, :, t * 512:(t + 1) * 512].rearrange("c p t -> p c t"),
                                                      in_=GAt[:]), [a], ev=ev_stga)
                        state["pending_stores"].append((t * NG + gi, mk))

            sp_xload(0)
            if NT > 1:
                sp_xload(1)
            act_prep(0)
            dve_prep(0)
            for s in range(4):
                pe_trh(0, s)
                act_trh_evac(0, s)
            for gi in range(min(3, NG)):
                sp_wload(0, gi)

            qk_pending = []
            for t in range(NT):
                for gi in range(NG):
                    gg = t * NG + gi
                    if GROUPS[gi][0] in ("q", "k"):
                        state["qkg", t, gi] = state.get("qkg_cnt", 0)
                        state["qkg_cnt"] = state.get("qkg_cnt", 0) + 1
                    nxt = gg + 3
                    if gi == 0 and t + 2 < NT and t >= 0:
                        pass
                    for s in range(4):
                        u = gi * 4 + s
                        if u == 38 and t + 1 < NT:
                            act_prep(t + 1)
                            dve_prep(t + 1)
                        if u == 56 and t + 1 < NT:
                            for s2 in range(4):
                                pe_trh(t + 1, s2)
                                act_trh_evac(t + 1, s2)
                        mm = pe_unit(t, gi, s)
                        kind = GROUPS[gi][0]
                        if kind in ("q", "k"):
                            n = consume_qk(t, gi, s, mm)
                            qk_pending.append((n, t, gi, s, t * 64 + u))
                        elif kind == "v":
                            consume_v(t, gi, s, mm)
                        else:
                            consume_conv(t, gi, s, mm)
                        cur = t * 64 + u
                        while qk_pending and qk_pending[0][4] + 6 <= cur:
                            n0, t0_, gi0, s0, _ = qk_pending.pop(0)
                            pe_trq(n0)
                            act_trq_evac(n0, t0_, gi0, s0)
                    if t == 0 and gi + 3 < NG:
                        pool_cast(gi + 3)
                    if nxt < NT * NG:
                        sp_wload(nxt // NG, nxt % NG)
                    if gi == 8 and t + 2 < NT:
                        sp_xload(t + 2)
                    keep = []
                    for (g0, mk) in state["pending_stores"]:
                        if g0 + 2 <= gg:
                            mk()
                        else:
                            keep.append((g0, mk))
                    state["pending_stores"] = keep
            while qk_pending:
                n0, t0_, gi0, s0, _ = qk_pending[0]
                if n0 not in state["pe_trq"]:
                    pe_trq(n0)
                qk_pending.pop(0)
                act_trq_evac(n0, t0_, gi0, s0)
            for (g0, mk) in state["pending_stores"]:
                mk()
            state["pending_stores"] = []
            fin_waits = [o for o in state["st_qk"] + state["st_v"] + [state["st_yc"], state["st_ga"]] if o is not None]
            SP.add(lambda e: e.nop(), fin_waits, ev=None)
            run_block(nc, st)

    def phase_B():
        NQ = S // 512
        with ExitStack() as ps:
            def sb(name, shape, dt=F32):
                return ps.enter_context(nc.sbuf_tensor("B_" + name, list(shape), dt))

            st = mk_streams(nc, es, "B")
            PE, ACT, DVE, POOL, SP = st["pe"], st["act"], st["dve"], st["pool"], st["sp"]
            ev_const = Ev(nc, es, "B_const", 16)
            ev_c2 = Ev(nc, es, "B_c2", 16)
            ev_ldq = [Ev(nc, es, f"B_ldq{i}", 16) for i in range(2)]
            ev_ldk = [Ev(nc, es, f"B_ldk{i}", 16) for i in range(2)]
            ev_ldv = [Ev(nc, es, f"B_ldv{i}", 16) for i in range(2)]
            ev_ldga = [Ev(nc, es, f"B_ldga{i}", 16) for i in range(2)]
            ev_ldyc = [Ev(nc, es, f"B_ldyc{i}", 16) for i in range(2)]
            ev_stmt = [Ev(nc, es, f"B_stmt{i}", 16) for i in range(2)]

            QTh = [sb(f"QTh{i}", [128, S], BF16) for i in range(2)]
            KTh = [sb(f"KTh{i}", [128, S], BF16) for i in range(2)]
            Vh = [sb(f"Vh{i}", [128, NCH, 130], BF16) for i in range(2)]
            NP = 6
            P = [sb(f"P{i}", [128, 2, 512], BF16) for i in range(NP)]
            tri = sb("tri", [128, 128], BF16)
            mneg = sb("mneg", [128, 128], BF16)
            ident = sb("ident", [128, 128], BF16)
            GAq = [sb(f"GAq{i}", [128, 512], BF16) for i in range(2)]
            YCq = [sb(f"YCq{i}", [128, 512], BF16) for i in range(2)]
            lam4 = sb("lam4", [128, 4, 64])
            ljunk = sb("ljunk", [128, 64])
            s12 = sb("s12", [128, 2])
            e12 = sb("e12", [128, 2])
            nlam = sb("nlam", [128, 1])
            subg = sb("subg", [128, 128])
            sg08 = sb("sg08", [128, 128])
            neghalf = sb("neghalf", [128, 4])
            rr9 = sb("rr9", [128, 3, 3])
            oraw = sb("oraw", [128, 3, 387])
            oa = sb("oa", [128, 4, 128])
            ob = sb("ob", [128, 4, 128])
            od = sb("od", [128, 4, 128])
            djunk = sb("djunk", [128, 128])
            ssd = sb("ssd", [128, 4])
            vard = sb("vard", [128, 4])
            rsd = sb("rsd", [128, 4])
            ofin = sb("ofin", [128, 4, 128], BF16)
            mtmp = sb("mtmp", [128, 512])
            mt = [sb(f"mt{i}", [128, 512], BF16) for i in range(2)]

            c1 = SP.add(lambda e: e.dma_start(out=lam4[:], in_=lam_d), ev=ev_const)
            c2 = SP.add(lambda e: e.dma_start(out=subg[:], in_=sub_d), ev=ev_const)
            p1 = POOL.add(lambda e: e.dma_start(out=tri[:], in_=tri_d), ev=ev_c2)
            POOL.add(lambda e: e.dma_start(out=mneg[:], in_=mneg_d), ev=ev_c2)
            p2 = POOL.add(lambda e: e.dma_start(out=ident[:], in_=ident_d), ev=ev_c2)
            for i in range(8):
                POOL.add(lambda e, i=i: e.dma_start(out=Wb_up[i * 128:(i + 1) * 128, :], in_=w_up[i * 128:(i + 1) * 128, :]),
                         ev=ev_wrest)
            for i in range(DFF // 128 // 2):
                POOL.add(lambda e, i=i: e.dma_start(out=Wb_dn[i * 256:(i + 1) * 256, :], in_=w_dn[i * 256:(i + 1) * 256, :]),
                         ev=ev_wrest)
            POOL.add(lambda e: e.dma_start(out=Wb_out[:, :], in_=w_out[:, :]), ev=ev_wrest)
            mv = []
            for i in range(2):
                mv.append(DVE.add(lambda e, i=i: e.memset(Vh[i][:, :, 128:130], 1.0)))
            mnh = DVE.add(lambda e: e.memset(neghalf[:], -0.5))
            for k in range(2):
                DVE.add(lambda e, k=k: e.scalar_tensor_tensor(out=ljunk[:], in0=lam4[:, 2 * k, :], scalar=1.0,
                                                            in1=lam4[:, 2 * k + 1, :], op0=ALU.mult, op1=ALU.mult,
                                                            accum_out=s12[:, k:k + 1]), [c1, c2], sig=False)
            dsg = DVE.add(lambda e: e.tensor_scalar(out=sg08[:], in0=subg[:], scalar1=1.0 - LAMBDA_INIT, scalar2=None,
                                                    op0=ALU.mult), [c1, c2])
            aexp = ACT.add(lambda e: e.activation(out=e12[:], in_=s12[:], func=AF.Exp), [dsg])
            dl0 = DVE.add(lambda e: e.tensor_tensor(out=nlam[:], in0=e12[:, 1:2], in1=e12[:, 0:1], op=ALU.subtract), [aexp])
            dlam = DVE.add(lambda e: e.tensor_scalar(out=nlam[:], in0=nlam[:], scalar1=-LAMBDA_INIT, scalar2=None,
                                                     op0=ALU.add), hard=[dl0])

            def acc(a):
                o = (a % 3) * 129
                return pall[:, 4 + a // 3, o:o + 129]

            chunks = []
            groups = []
            for h in range(NH):
                for qb in range(NQ):
                    g = len(groups)
                    first = len(chunks)
                    for kc in range(4 * qb + 4):
                        chunks.append((h, qb, kc, kc - 4 * qb, g))
                    groups.append((h, qb, first, len(chunks) - 1))
            NCk = len(chunks)
            NG = len(groups)

            S_ = dict(qk={}, exp={}, pready={}, pv={}, head_ld={}, last_pv_head={}, evac1={}, ofin={}, trg={},
                      merge={}, ldga={}, st={}, mask={})

            def sp_head_load(h):
                hp = h % 2
                w = []
                if h >= 2:
                    w.append(S_["last_pv_head"][h - 2])
                a = SP.add(lambda e: e.dma_start(out=QTh[hp][:], in_=QT[h]), w, ev=ev_ldq[hp])
                b = SP.add(lambda e: e.dma_start(out=KTh[hp][:], in_=KT[h]), w, ev=ev_ldk[hp])
                c = SP.add(lambda e: e.dma_start(out=Vh[hp][:, :, 0:128], in_=Vs[h]), w + mv, ev=ev_ldv[hp])
                S_["head_ld"][h] = [a, b, c]

            def sp_group_load(g):
                h, qb, _, _ = groups[g]
                sl = g % 2
                w = []
                if g >= 2:
                    w.append(S_["merge"][g - 2])
                a = SP.add(lambda e: e.dma_start(out=GAq[sl][:], in_=GA[h, :, qb * 512:(qb + 1) * 512]), w, ev=ev_ldga[sl])
                b = SP.add(lambda e: e.dma_start(out=YCq[sl][:], in_=YC[h, :, qb * 512:(qb + 1) * 512]), w, ev=ev_ldyc[sl])
                S_["ldga"][g] = [a, b]

            def sp_group_store(g):
                h, qb, _, _ = groups[g]
                sl = g % 2
                S_["st"][g] = SP.add(lambda e: e.dma_start(out=MT[h, :, qb * 512:(qb + 1) * 512], in_=mt[sl][:]),
                                     [S_["merge"][g]], ev=ev_stmt[sl])

            def pe_qk(i):
                h, qb, kc, j, g = chunks[i]
                hp = h % 2
                sbuf_i = i % 2
                c0 = 128 * max(j, 0)
                w = list(S_["head_ld"][h][0:2])
                if i >= 2:
                    w.append(S_["exp"][i - 2])

                def fn(e):
                    ins = None
                    for m in range(2):
                        ins = e.matmul(pall[:, sbuf_i * 2 + m, c0:512],
                                       lhsT=KTh[hp][64 * m:64 * m + 64, kc * 128:(kc + 1) * 128],
                                       rhs=QTh[hp][64 * m:64 * m + 64, qb * 512 + c0:(qb + 1) * 512],
                                       start=True, stop=True)
                    if j >= 0:
                        for m in range(2):
                            ins = e.matmul(pall[:, sbuf_i * 2 + m, c0:c0 + 128], lhsT=ident[:], rhs=mneg[:],
                                           start=False, stop=True, skip_group_check=True)
                    return ins
                if j >= 0:
                    w.append(p2)
                S_["qk"][i] = PE.add(fn, w)

            def act_exp(i):
                h, qb, kc, j, g = chunks[i]
                sbuf_i = i % 2
                pb = i % NP
                c0 = 128 * max(j, 0)
                w = [S_["qk"][i]]
                if i >= NP:
                    w.append(S_["pv"][i - NP])
                op = ACT.add(lambda e: e.activation(out=P[pb][:, :, c0:512], in_=pall[:, sbuf_i * 2:sbuf_i * 2 + 2, c0:512],
                                                    func=AF.Exp, scale=0.125), w)
                S_["exp"][i] = op
                S_["pready"][i] = op

            def pe_pv(i):
                h, qb, kc, j, g = chunks[i]
                hp = h % 2
                pb = i % NP
                w = [S_["pready"][i], S_["head_ld"][h][2]]
                if kc == 0 and g >= 1:
                    w.append(S_["evac1"][g - 1])

                def fn(e):
                    ins = None
                    for t in range(max(j, 0), 4):
                        for m in range(2):
                            a = t * 2 + m
                            ins = e.matmul(acc(a), lhsT=P[pb][:, m, t * 128:(t + 1) * 128], rhs=Vh[hp][:, kc, 0:129],
                                           start=(kc == 0 and a % 3 == 0), stop=(kc == 4 * qb + t),
                                           skip_group_check=True)
                    return ins
                op = PE.add(fn, w)
                S_["pv"][i] = op
                S_["last_pv_head"][h] = op

            def dve_evac(g):
                h, qb, first, last = groups[g]
                w = [S_["pv"][last], dlam, mnh]
                cp = DVE.add(lambda e: e.tensor_copy(out=oraw[:], in_=pall[:, 4:7, 0:387]), w)
                S_["evac1"][g] = cp
                rc = DVE.add(lambda e: e.reciprocal(out=rr9[:], in_=oraw[:, :, 128:387:129]))

                def accs(a):
                    o = (a % 3) * 129
                    return oraw[:, a // 3, o:o + 128]
                op = None
                for t in range(4):
                    a0, a1 = 2 * t, 2 * t + 1
                    DVE.add(lambda e, t=t, a0=a0: e.tensor_scalar(out=oa[:, t, :], in0=accs(a0),
                                                                  scalar1=rr9[:, a0 // 3, a0 % 3:a0 % 3 + 1], scalar2=None,
                                                                  op0=ALU.mult), sig=False, hard=[rc] if t == 0 else [])
                    op = DVE.add(lambda e, t=t, a1=a1: e.tensor_scalar(out=ob[:, t, :], in0=accs(a1),
                                                                       scalar1=rr9[:, a1 // 3, a1 % 3:a1 % 3 + 1],
                                                                       scalar2=None, op0=ALU.mult), sig=(t == 3))
                odl = None
                for t in range(4):
                    odl = DVE.add(lambda e, t=t: e.scalar_tensor_tensor(out=od[:, t, :], in0=ob[:, t, :], scalar=nlam[:, 0:1],
                                                                       in1=oa[:, t, :], op0=ALU.mult, op1=ALU.add),
                                  hard=[op] if t == 0 else [])
                ssl = None
                for t in range(4):
                    ssl = DVE.add(lambda e, t=t: e.scalar_tensor_tensor(out=djunk[:], in0=od[:, t, :], scalar=1.0,
                                                                       in1=od[:, t, :], op0=ALU.mult, op1=ALU.mult,
                                                                       accum_out=ssd[:, t:t + 1]),
                                  hard=[odl] if t == 0 else [])
                dv = DVE.add(lambda e: e.tensor_scalar(out=vard[:], in0=ssd[:], scalar1=1.0 / 128, scalar2=EPS,
                                                       op0=ALU.mult, op1=ALU.add), hard=[ssl])
                pw = POOL.add(lambda e: e.tensor_tensor(out=rsd[:], in0=vard[:], in1=neghalf[:], op=ALU.pow), [dv, mnh])
                w = [pw]
                if g >= 1:
                    w.append(S_["trg"][g - 1])
                op = None
                for t in range(4):
                    op = DVE.add(lambda e, t=t: e.scalar_tensor_tensor(out=ofin[:, t, :], in0=od[:, t, :],
                                                                      scalar=rsd[:, t:t + 1], in1=sg08[:],
                                                                      op0=ALU.mult, op1=ALU.mult), w, sig=(t == 3))
                S_["ofin"][g] = op

            def pe_trg(g):
                w = [S_["ofin"][g], p2]
                if g >= 1:
                    w.append(S_["merge"][g - 1])

                def fn(e):
                    ins = None
                    for t in range(4):
                        ins = e.transpose(out=bank_bf(7)[:, t * 128:(t + 1) * 128], in_=ofin[:, t, :], identity=ident[:])
                    return ins
                S_["trg"][g] = PE.add(fn, w)

            def dve_merge(g):
                sl = g % 2
                w = [S_["trg"][g]] + S_["ldga"][g]
                if g >= 2:
                    w.append(S_["st"][g - 2])
                DVE.add(lambda e: e.tensor_tensor(out=mtmp[:], in0=bank_bf(7)[:, 0:512], in1=GAq[sl][:], op=ALU.mult), w,
                        sig=False)
                S_["merge"][g] = DVE.add(lambda e: e.tensor_tensor(out=mt[sl][:], in0=mtmp[:], in1=YCq[sl][:], op=ALU.add))

            sp_head_load(0)
            sp_group_load(0)
            if NG > 1:
                sp_group_load(1)
            pe_qk(0)
            act_exp(0)
            if NCk > 1:
                pe_qk(1)
                act_exp(1)
            for i in range(NCk):
                h, qb, kc, j, g = chunks[i]
                _, _, first, last = groups[g]
                if i + 2 < NCk:
                    pe_qk(i + 2)
                    act_exp(i + 2)
                if i >= 1:
                    pe_pv(i - 1)
                    gp = chunks[i - 1][4]
                    if i - 1 == groups[gp][3]:
                        dve_evac(gp)
                if i == first and qb == 0 and h + 1 < NH:
                    sp_head_load(h + 1)
                if g >= 1 and i == min(first + 9, last):
                    pe_trg(g - 1)
                    dve_merge(g - 1)
                    sp_group_store(g - 1)
                    if g + 1 < NG:
                        sp_group_load(g + 1)
            pe_pv(NCk - 1)
            dve_evac(NG - 1)
            pe_trg(NG - 1)
            dve_merge(NG - 1)
            sp_group_store(NG - 1)
            SP.add(lambda e: e.nop(), [S_["st"][g] for g in range(max(0, NG - 2), NG)], ev=None)
            run_block(nc, st)

    def phase_C():
        with ExitStack() as ps:
            def sb(name, shape, dt=F32):
                return ps.enter_context(nc.sbuf_tensor("C_" + name, list(shape), dt))

            st = mk_streams(nc, es, "C")
            PE, ACT, DVE, POOL, SP = st["pe"], st["act"], st["dve"], st["pool"], st["sp"]
            ev_const = Ev(nc, es, "C_const", 16)
            ev_c2 = Ev(nc, es, "C_c2", 16)
            ev_ldx = [Ev(nc, es, f"C_ldx{i}", 16) for i in range(2)]
            ev_ldm = [Ev(nc, es, f"C_ldm{i}", 16) for i in range(1)]
            ev_ldu = [Ev(nc, es, f"C_ldu{i}", 16) for i in range(3)]
            ev_ldd = [Ev(nc, es, f"C_ldd{i}", 16) for i in range(4)]
            ev_sto = [Ev(nc, es, f"C_sto{i}", 16) for i in range(2)]

            x1 = [sb(f"x1{i}", [128, 4, D]) for i in range(2)]
            MTt = [sb(f"MTt{i}", [128, 8, 512], BF16) for i in range(1)]
            Wo = sb("Wo", [128, 8, D], BF16)
            Wu = [sb(f"Wu{i}", [128, 8, 2, 256], BF16) for i in range(3)]
            Wd = [sb(f"Wd{i}", [128, 11, 512], BF16) for i in range(4)]
            hb = sb("hb", [128, 4, D], BF16)
            hT = sb("hT", [128, 8, 512], BF16)
            actT = sb("actT", [128, 22, 512], BF16)
            g2bc = sb("g2bc", [128, D])
            fcw = sb("fcw", [128, 3, 44])
            fcb = sb("fcb", [128, 44])
            ident = sb("ident", [128, 128], BF16)
            neghalf = sb("neghalf", [128, 4])
            ss4 = sb("ss4", [128, 4])
            var4 = sb("var4", [128, 4])
            rstd4 = sb("rstd4", [128, 4])
            junk = sb("junk", [128, D], BF16)
            halo = sb("halo", [128, 44, 2])
            NU = 6
            NY = 3
            ub = [sb(f"ub{i}", [128, 514]) for i in range(NU)]
            ya = [sb(f"ya{i}", [128, 512]) for i in range(NY)]
            yb = [sb(f"yb{i}", [128, 512]) for i in range(NY)]
            sa = [sb(f"sa{i}", [128, 512]) for i in range(NY)]

            NTl = S // 512
            c_list = []
            for dst, src in [(g2bc, g2bc_d), (fcw, fcw_d), (fcb, fcb_d)]:
                c_list.append(SP.add(lambda e, dst=dst, src=src: e.dma_start(out=dst[:], in_=src), ev=ev_const))
            p_id = POOL.add(lambda e: e.dma_start(out=ident[:], in_=ident_d), ev=ev_c2)
            class _W:
                pass
            wrest = Op(None, [], ev_wrest, None)
            wrest.val = ev_wrest.n
            wo_ld = SP.add(lambda e: e.dma_start(out=Wo[:], in_=Wb_out.rearrange("(kc p) n -> p kc n", p=128)), [wrest],
                           ev=ev_const)
            c_list.append(wo_ld)
            mh = DVE.add(lambda e: e.memset(halo[:], 0.0))
            mnh = DVE.add(lambda e: e.memset(neghalf[:], -0.5))

            x_v = x.rearrange("(t s p) d -> t p s d", s=4, p=128)
            o_v = out.rearrange("(t s p) d -> t p s d", s=4, p=128)
            Wuv = Wb_up.rearrange("(kc p) (ab n) -> p kc ab n", p=128, ab=2)
            Wdv = Wb_dn.rearrange("(c p) n -> p c n", p=128)

            C_ = dict(xld={}, mld={}, sto=[None, None], uld={}, dld={}, rd={}, unit_cnt=0, last_pe_tile={},
                      trh_evac=[], pe_trh={}, x1add={}, hbops={}, hT_ready={}, last_read_hT={}, act_ready={},
                      ug_last={}, dg_last={}, halo_rd=[mh] * 44, ub_free=[None] * NU, ucnt=0, pair_cnt=0,
                      ya_free=[None] * 3, yb_free=[None] * 3, sa_free=[None] * 3, lag=None, fin_add={}, down_last={},
                      wo_done={}, act_sq={})

            def sp_xload(t):
                sl = t % 2
                w = []
                if t >= 2:
                    w.append(C_["sto"][sl])
                C_["xld"][t] = SP.add(lambda e: e.dma_start(out=x1[sl][:], in_=x_v[t]), w, ev=ev_ldx[sl])

            def sp_mload(t):
                w = []
                if t >= 1:
                    w.append(C_["wo_done"][t - 1])
                C_["mld"][t] = SP.add(
                    lambda e: e.dma_start(out=MTt[0][:], in_=MT[:, :, t * 512:(t + 1) * 512].rearrange("c p t -> p c t")),
                    w, ev=ev_ldm[0])

            def sp_uload(t, gi):
                gg = t * 11 + gi
                sl = gg % 3
                w = [wrest]
                if gg >= 3:
                    w.append(C_["ug_last"][gg - 3])
                for ab in range(2):
                    C_["uld"][gg] = SP.add(
                        lambda e, ab=ab: e.dma_start(out=Wu[sl][:, :, ab, :], in_=Wuv[:, :, ab, gi * 256:(gi + 1) * 256]), w,
                        ev=ev_ldu[sl])

            def sp_dload(t, di):
                gg = t * 4 + di
                sl = gg % 4
                cg, hf = di // 2, di % 2
                w = [wrest]
                if gg >= 4:
                    w.append(C_["dg_last"][gg - 4])
                C_["dld"][gg] = SP.add(
                    lambda e: e.dma_start(out=Wd[sl][:], in_=Wdv[:, hf * 11:(hf + 1) * 11, cg * 512:(cg + 1) * 512]), w,
                    ev=ev_ldd[sl])

            def next_bank():
                n = C_["unit_cnt"]
                C_["unit_cnt"] += 1
                return n, n % 5

            def bank_wait(n):
                return C_["rd"][n - 5] if n >= 5 else []

            def st_wout(t):
                sl = t % 2
                for s in range(4):
                    for cg in range(2):
                        n, bk = next_bank()
                        w = [C_["mld"][t], wo_ld] + bank_wait(n)

                        def fn(e, s=s, cg=cg, bk=bk):
                            ins = None
                            for kc in range(8):
                                ins = e.matmul(pall[:, bk, :], lhsT=MTt[0][:, kc, s * 128:(s + 1) * 128],
                                               rhs=Wo[:, kc, cg * 512:(cg + 1) * 512], start=(kc == 0), stop=(kc == 7))
                            return ins
                        mm = PE.add(fn, w)
                        C_["wo_done"][t] = mm
                        d = DVE.add(lambda e, s=s, cg=cg, bk=bk: e.tensor_tensor(
                            out=x1[sl][:, s, cg * 512:(cg + 1) * 512], in0=pall[:, bk, :],
                            in1=x1[sl][:, s, cg * 512:(cg + 1) * 512], op=ALU.add), [mm, C_["xld"][t]])
                        C_["rd"][n] = [d]
                        C_["x1add"][t, s] = d

            def st_norm(t):
                sl = t % 2
                aop = None
                for s in range(4):
                    aop = ACT.add(lambda e, s=s: e.activation(out=junk[:], in_=x1[sl][:, s, :], func=AF.Square,
                                                              accum_out=ss4[:, s:s + 1]), [C_["x1add"][t, s]])
                dv = DVE.add(lambda e: e.tensor_scalar(out=var4[:], in0=ss4[:], scalar1=1.0 / D, scalar2=EPS,
                                                       op0=ALU.mult, op1=ALU.add), [aop])
                pw = POOL.add(lambda e: e.tensor_tensor(out=rstd4[:], in0=var4[:], in1=neghalf[:], op=ALU.pow), [dv, mnh])
                hbo = []
                C_["hbo", t] = hbo
                for s in range(4):
                    w = [pw] + c_list
                    if t >= 1:
                        w.append(C_["pe_trh"][t - 1, 3])
                    hbo.append(DVE.add(lambda e, s=s: e.scalar_tensor_tensor(
                        out=hb[:, s, :], in0=x1[sl][:, s, :], scalar=rstd4[:, s:s + 1], in1=g2bc[:],
                        op0=ALU.mult, op1=ALU.mult), w))

            def st_trans(t):
                sl = t % 2
                for s in range(4):
                    bkt = 5 + (s % 2)
                    w = [C_["hbo", t][s], p_id]
                    if len(C_["trh_evac"]) >= 2:
                        w.append(C_["trh_evac"][-2])

                    def fn(e, s=s, bkt=bkt):
                        ins = None
                        for kc in range(8):
                            ins = e.transpose(out=bank_bf(bkt)[:, kc * 128:(kc + 1) * 128],
                                              in_=hb[:, s, kc * 128:(kc + 1) * 128], identity=ident[:])
                        return ins
                    pt = PE.add(fn, w)
                    C_["pe_trh"][t, s] = pt
                    w = [pt]
                    if s == 0 and t >= 1:
                        w.append(C_["last_read_hT"][t - 1])
                    ev = ACT.add(lambda e, s=s, bkt=bkt: e.copy(out=hT[:, :, s * 128:(s + 1) * 128],
                                                                 in_=bank_bf(bkt).rearrange("p (k t) -> p k t", k=8)), w)
                    C_["trh_evac"].append(ev)
                C_["hT_ready"][t] = C_["trh_evac"][-1]

            def st_up(t):
                sl = t % 2
                hT_ready = C_["hT_ready"][t]
                for gi in range(11):
                    gg = t * 11 + gi
                    usl = gg % 3
                    for cl in range(2):
                        c = gi * 2 + cl
                        pc = C_["pair_cnt"]
                        C_["pair_cnt"] += 1
                        wsl = pc % NY
                        res = {}
                        for ab in range(2):
                            ch = c + 22 * ab
                            n, bk = next_bank()
                            w = [C_["uld"][gg], hT_ready] + bank_wait(n)

                            def fn(e, ab=ab, cl=cl, bk=bk, usl=usl):
                                ins = None
                                for kc in range(8):
                                    ins = e.matmul(pall[:, bk, :], lhsT=Wu[usl][:, kc, ab, cl * 128:(cl + 1) * 128],
                                                   rhs=hT[:, kc, :], start=(kc == 0), stop=(kc == 7))
                                return ins
                            mm = PE.add(fn, w)
                            C_["ug_last"][gg] = mm
                            C_["last_read_hT"][t] = mm
                            un = C_["ucnt"]
                            C_["ucnt"] += 1
                            ui = un % NU
                            w = [C_["halo_rd"][ch]]
                            if C_["ub_free"][ui] is not None:
                                w.append(C_["ub_free"][ui])
                            ACT.add(lambda e, ui=ui, ch=ch: e.copy(out=ub[ui][:, 0:2], in_=halo[:, ch, :]), w, sig=False)
                            ev1 = ACT.add(lambda e, ui=ui, bk=bk: e.copy(out=ub[ui][:, 2:514], in_=pall[:, bk, :]), [mm])
                            ydst = (ya if ab == 0 else yb)[wsl]
                            yfree = (C_["ya_free"] if ab == 0 else C_["yb_free"])[wsl]
                            w = c_list[:]
                            if yfree is not None:
                                w.append(yfree)
                            tap0 = ACT.add(lambda e, bk=bk, ch=ch, ydst=ydst: e.activation(
                                out=ydst[:], in_=pall[:, bk, :], func=AF.Identity, scale=fcw[:, 2, ch:ch + 1],
                                bias=fcb[:, ch:ch + 1]), w)
                            hs = ACT.add(lambda e, bk=bk, ch=ch: e.copy(out=halo[:, ch, :], in_=pall[:, bk, 510:512]))
                            C_["rd"][n] = [hs]
                            C_["halo_rd"][ch] = hs
                            DVE.add(lambda e, ui=ui, ch=ch, ydst=ydst: e.scalar_tensor_tensor(
                                out=ydst[:], in0=ub[ui][:, 1:513], scalar=fcw[:, 1, ch:ch + 1], in1=ydst[:],
                                op0=ALU.mult, op1=ALU.add), [tap0, ev1], sig=False)
                            cv = DVE.add(lambda e, ui=ui, ch=ch, ydst=ydst: e.scalar_tensor_tensor(
                                out=ydst[:], in0=ub[ui][:, 0:512], scalar=fcw[:, 0, ch:ch + 1], in1=ydst[:],
                                op0=ALU.mult, op1=ALU.add))
                            C_["ub_free"][ui] = cv
                            res[ab] = cv
                        def fin_pair(t=t, c=c, wsl=wsl, res=res):
                            w = [res[0]]
                            if C_["sa_free"][wsl] is not None:
                                w.append(C_["sa_free"][wsl])
                            si = ACT.add(lambda e: e.activation(out=sa[wsl][:], in_=ya[wsl][:], func=AF.Silu), w)
                            C_["ya_free"][wsl] = si
                            w = [si, res[1]]
                            if t >= 1:
                                w.append(C_["down_last"][t - 1])
                            pr = DVE.add(lambda e: e.tensor_tensor(out=actT[:, c, :], in0=sa[wsl][:], in1=yb[wsl][:],
                                                                   op=ALU.mult), w)
                            C_["sa_free"][wsl] = pr
                            C_["yb_free"][wsl] = pr
                            C_["act_ready"][t, c] = pr
                        if C_["lag"] is not None:
                            C_["lag"]()
                        C_["lag"] = fin_pair
                    nxt = gg + 3
                    if nxt < NTl * 11:
                        sp_uload(nxt // 11, nxt % 11)
                C_["lag"]()
                C_["lag"] = None

            def st_down(t, cg):
                sl = t % 2
                if True:
                    for s in range(4):
                        n, bk = next_bank()
                        w = bank_wait(n) + [C_["act_ready"][t, 21], C_["dld"][t * 4 + cg * 2], C_["dld"][t * 4 + cg * 2 + 1]]

                        def fn(e, s=s, cg=cg, bk=bk):
                            ins = None
                            for c in range(22):
                                dsl = (t * 4 + cg * 2 + c // 11) % 4
                                ins = e.matmul(pall[:, bk, :], lhsT=actT[:, c, s * 128:(s + 1) * 128],
                                               rhs=Wd[dsl][:, c % 11, :], start=(c == 0), stop=(c == 21))
                            return ins
                        mm = PE.add(fn, w)
                        C_["down_last"][t] = mm
                        C_["dg_last"][t * 4 + cg * 2] = mm
                        C_["dg_last"][t * 4 + cg * 2 + 1] = mm
                        d = DVE.add(lambda e, s=s, cg=cg, bk=bk: e.tensor_tensor(
                            out=x1[sl][:, s, cg * 512:(cg + 1) * 512], in0=pall[:, bk, :],
                            in1=x1[sl][:, s, cg * 512:(cg + 1) * 512], op=ALU.add), [mm])
                        C_["rd"][n] = [d]
                        C_["fin_add"][t] = d
                    for k in range(2):
                        nxt = t * 4 + cg * 2 + k + 4
                        if nxt < NTl * 4 and nxt not in C_["dld"]:
                            sp_dload(nxt // 4, nxt % 4)

            def st_store(t):
                sl = t % 2
                C_["sto"][sl] = SP.add(lambda e: e.dma_start(out=o_v[t], in_=x1[sl][:]), [C_["fin_add"][t]], ev=ev_sto[sl])
                if t + 2 < NTl:
                    sp_xload(t + 2)


            sp_xload(0)
            if NTl > 1:
                sp_xload(1)
            sp_mload(0)
            for gi in range(3):
                sp_uload(0, gi)
            for di in range(4):
                sp_dload(0, di)
            st_wout(0)
            if NTl > 1:
                sp_mload(1)
            st_norm(0)
            st_trans(0)
            for t in range(NTl):
                st_up(t)
                if t + 1 < NTl:
                    st_wout(t + 1)
                    if t + 2 < NTl:
                        sp_mload(t + 2)
                    st_norm(t + 1)
                st_down(t, 0)
                if t + 1 < NTl:
                    st_trans(t + 1)
                st_down(t, 1)
                st_store(t)
            SP.add(lambda e: e.nop(), [o for o in C_["sto"] if o is not None], ev=None)
            run_block(nc, st)

    if "A" in phases:
        phase_A()
    if "B" in phases:
        phase_B()
    if "C" in phases:
        phase_C()
    es.close()
    return nc


def host_constants(S):
    nch = S // 128
    inv = (np.float32(10000.0) ** (-(np.arange(0, 64, 2, dtype=np.float32)) / np.float32(64))).astype(np.float32)
    ang = (np.arange(S, dtype=np.float32)[:, None] * inv[None, :]).astype(np.float32)
    cos = np.cos(ang).astype(np.float32)
    sin = np.sin(ang).astype(np.float32)
    cosT = np.ascontiguousarray(cos.reshape(nch, 128, 32).transpose(1, 0, 2))
    sinT = np.ascontiguousarray(sin.reshape(nch, 128, 32).transpose(1, 0, 2))
    ident = np.eye(128, dtype=np.float32)
    tri = np.triu(np.ones((128, 128), dtype=np.float32))
    mneg = np.where(np.arange(128)[:, None] > np.arange(128)[None, :], np.float32(-30000.0), np.float32(0.0)).astype(np.float32)
    return dict(cosT=cosT, sinT=sinT, ident=ident, tri=tri, mneg=mneg)


def bc(v, n=128):
    return np.ascontiguousarray(np.broadcast_to(np.asarray(v, dtype=np.float32).reshape(1, -1), (n, v.size)))


def make_in_maps(inputs, S):
    f = lambda a: np.ascontiguousarray(np.asarray(a, dtype=np.float32))
    cst = host_constants(S)
    shared = dict(
        w_in=f(inputs["w_in"][0]), w_out=f(inputs["w_out"][0]), w_up=f(inputs["w_up"][0]), w_dn=f(inputs["w_down"][0]),
        g1bc=bc(f(inputs["attn_norm_g"][0])), g2bc=bc(f(inputs["ffn_norm_g"][0])),
        qg=bc(f(inputs["q_norm_g"][0])), kg=bc(f(inputs["k_norm_g"][0])),
        lam4=np.ascontiguousarray(np.stack([bc(f(inputs[k][0])) for k in ("lambda_q1", "lambda_k1", "lambda_q2", "lambda_k2")],
                                           axis=1)),
        subg=bc(f(inputs["subln_g"][0])),
        bg=np.ascontiguousarray(f(inputs["b_gate"][0]).reshape(16, 128).T),
        scw=np.ascontiguousarray(f(inputs["short_conv_w"][0]).reshape(3, 8, 128).transpose(2, 0, 1)),
        fcw=np.ascontiguousarray(f(inputs["ffn_conv_w"][0]).reshape(3, 44, 128).transpose(2, 0, 1)),
        fcb=np.ascontiguousarray(f(inputs["ffn_conv_b"][0]).reshape(44, 128).T),
        **cst,
    )
    xs = f(inputs["x"])
    maps = []
    for b in range(xs.shape[0]):
        m = dict(shared)
        m["x"] = np.ascontiguousarray(xs[b])
        maps.append(m)
    return maps


_PROG_CACHE = {}


def kernel(**inputs):
    x = np.asarray(inputs["x"])
    B, S, _ = x.shape
    assert B == NCORES
    if S not in _PROG_CACHE:
        _PROG_CACHE[S] = build_program(S)
    nc = _PROG_CACHE[S]
    in_maps = make_in_maps(inputs, S)
    res = run_bass_kernel_spmd(nc, in_maps, core_ids=list(range(NCORES)))
    return np.stack([np.asarray(r["out"], dtype=np.float32) for r in res.results], axis=0)
```

```python
import math
from contextlib import ExitStack

import numpy as np
import concourse.bass as bass
import concourse.mybir as mybir
from concourse.bass_utils import run_bass_kernel_spmd

F32 = mybir.dt.float32
BF16 = mybir.dt.bfloat16
AF = mybir.ActivationFunctionType
ALU = mybir.AluOpType
AX = mybir.AxisListType

D = 1024
NH = 8
DFF = 2816
DIN = 8192
EPS = 1e-6
LAMBDA_INIT = 0.8 - 0.6 * math.exp(0.0)
NCORES = 8


class Ev:
    def __init__(self, nc, es, name, step=1):
        self.sem = es.enter_context(nc.semaphore(name))
        self.n = 0
        self.step = step


class Op:
    __slots__ = ("fn", "waits", "ev", "val", "stream", "hard")

    def __init__(self, fn, waits, ev, stream, hard=()):
        self.fn = fn
        self.hard = [w for w in hard if w is not None]
        self.waits = [w for w in waits if w is not None] + self.hard
        self.ev = ev
        self.val = None
        self.stream = stream


class Stream:
    def __init__(self, name, ev=None):
        self.name = name
        self.ops = []
        self.ev = ev

    def add(self, fn, waits=(), ev="default", sig=True, hard=()):
        if ev == "default":
            ev = self.ev if sig else None
        op = Op(fn, list(waits), ev, self, hard)
        self.ops.append(op)
        return op


def assign_values(streams):
    for s in streams:
        for op in s.ops:
            if op.ev is not None:
                op.ev.n += op.ev.step
                op.val = op.ev.n


def emit_stream(stream, eng):
    waited = {}
    for op in stream.ops:
        need = {}
        for w in op.waits:
            if w.ev is None:
                raise RuntimeError("waiting on op without event")
            if w.stream is stream and w.ev is stream.ev and not any(w is h for h in op.hard):
                continue
            if need.get(w.ev, 0) < w.val:
                need[w.ev] = w.val
        for ev, val in need.items():
            if waited.get(ev, 0) < val:
                eng.wait_ge(ev.sem, val)
                waited[ev] = val
        ins = op.fn(eng)
        if op.ev is not None:
            ins.then_inc(op.ev.sem, op.ev.step)


def run_block(nc, streams):
    assign_values(list(streams.values()))
    with nc.Block() as block:
        if "pe" in streams:
            @block.tensor
            def _(e):
                emit_stream(streams["pe"], e)
        if "act" in streams:
            @block.scalar
            def _(e):
                emit_stream(streams["act"], e)
        if "dve" in streams:
            @block.vector
            def _(e):
                emit_stream(streams["dve"], e)
        if "pool" in streams:
            @block.gpsimd
            def _(e):
                emit_stream(streams["pool"], e)
        if "sp" in streams:
            @block.sync
            def _(e):
                emit_stream(streams["sp"], e)


def mk_streams(nc, es, tag):
    st = {}
    for nm in ("pe", "act", "dve", "pool"):
        st[nm] = Stream(nm, Ev(nc, es, f"{tag}_{nm}"))
    st["sp"] = Stream("sp", None)
    return st


def build_program(S, debug=False, phases="ABC"):
    NT = S // 512
    NCH = S // 128
    nc = bass.Bass("TRN2", target_bir_lowering=False)
    es = ExitStack()

    def din(name, shape, dt=F32):
        return nc.dram_tensor(name, list(shape), dt, kind="ExternalInput").ap()

    okind = "ExternalOutput" if debug else "Internal"

    def dscr(name, shape, dt=BF16):
        return nc.dram_tensor(name, list(shape), dt, kind=okind).ap()

    x = din("x", [S, D])
    w_in = din("w_in", [D, DIN])
    w_out = din("w_out", [D, D])
    w_up = din("w_up", [D, 2 * DFF])
    w_dn = din("w_dn", [DFF, D])
    g1bc_d = din("g1bc", [128, D])
    g2bc_d = din("g2bc", [128, D])
    qg_d = din("qg", [128, 64])
    kg_d = din("kg", [128, 64])
    lam_d = din("lam4", [128, 4, 64])
    sub_d = din("subg", [128, 128])
    bg_d = din("bg", [128, 16])
    scw_d = din("scw", [128, 3, 8])
    fcw_d = din("fcw", [128, 3, 44])
    fcb_d = din("fcb", [128, 44])
    cos_d = din("cosT", [128, NCH, 32])
    sin_d = din("sinT", [128, NCH, 32])
    ident_d = din("ident", [128, 128])
    mneg_d = din("mneg", [128, 128])
    out = nc.dram_tensor("out", [S, D], F32, kind="ExternalOutput").ap()

    Wb_in = dscr("Wb_in", [D, DIN]) if not debug else nc.dram_tensor("Wb_in", [D, DIN], BF16, kind="Internal").ap()
    Wb_out = nc.dram_tensor("Wb_out", [D, D], BF16, kind="Internal").ap()
    Wb_up = nc.dram_tensor("Wb_up", [D, 2 * DFF], BF16, kind="Internal").ap()
    Wb_dn = nc.dram_tensor("Wb_dn", [DFF, D], BF16, kind="Internal").ap()
    QT = dscr("QT", [NH, 128, S])
    KT = dscr("KT", [NH, 128, S])
    Vs = dscr("Vs", [NH, 128, NCH, 128])
    GA = dscr("GA", [NH, 128, S])
    YC = dscr("YC", [NH, 128, S])
    MT = dscr("MT", [NH, 128, S])

    pall = es.enter_context(nc.psum_tensor("pall", [128, 8, 512], F32))

    def bank_bf(b):
        return pall[:, b, :].bitcast(BF16)

    ev_wrest = Ev(nc, es, "wrest", 16)

    def phase_A():
        with ExitStack() as ps:
            def sb(name, shape, dt=F32):
                return ps.enter_context(nc.sbuf_tensor("A_" + name, list(shape), dt))

            st = mk_streams(nc, es, "A")
            PE, ACT, DVE, POOL, SP = st["pe"], st["act"], st["dve"], st["pool"], st["sp"]
            ev_const = Ev(nc, es, "A_const", 16)
            ev_win = Ev(nc, es, "A_win", 16)
            ev_ldx = [Ev(nc, es, f"A_ldx{i}", 16) for i in range(2)]
            ev_ldw = [Ev(nc, es, f"A_ldw{i}", 16) for i in range(3)]
            ev_stqk = [Ev(nc, es, f"A_stqk{i}", 16) for i in range(2)]
            ev_stv = [Ev(nc, es, f"A_stv{i}", 16) for i in range(8)]
            ev_styc = Ev(nc, es, "A_styc", 16)
            ev_stga = Ev(nc, es, "A_stga", 16)

            xin = [sb(f"xin{i}", [128, 4, D]) for i in range(2)]
            hb = sb("hb", [128, 4, D], BF16)
            hT = [sb(f"hT{i}", [128, 8, 512], BF16) for i in range(2)]
            Wb = [sb(f"W{i}", [128, 8, 512], BF16) for i in range(3)]
            cosT = sb("cos", [128, NCH, 32])
            sinT = sb("sin", [128, NCH, 32])
            g1bc = sb("g1bc", [128, D])
            qg = sb("qg", [128, 64])
            kg = sb("kg", [128, 64])
            neghalf = sb("neghalf", [128, 8])
            ss4 = sb("ss4", [128, 4])
            var4 = sb("var4", [128, 4])
            rstd4 = sb("rstd4", [128, 4])
            junk = sb("junk", [128, D])
            sq = [sb(f"sq{i}", [128, 512]) for i in range(2)]
            t0 = [sb(f"t0{i}", [128, 512]) for i in range(2)]
            tA = [sb(f"tA{i}", [128, 512]) for i in range(2)]
            tB = [sb(f"tB{i}", [128, 512]) for i in range(2)]
            oo = [sb(f"oo{i}", [128, 512]) for i in range(2)]
            ssq = [sb(f"ssq{i}", [128, 8]) for i in range(2)]
            var8 = [sb(f"var8{i}", [128, 8]) for i in range(2)]
            rs8 = [sb(f"rs8{i}", [128, 8]) for i in range(2)]
            qn = [sb(f"qn{i}", [128, 512], BF16) for i in range(8)]
            QTt = [sb(f"QTt{i}", [128, 4, 512], BF16) for i in range(2)]
            vt = [sb(f"vt{i}", [128, 512], BF16) for i in range(8)]
            ubuf = sb("ubuf", [128, 8, 514])
            ycv = sb("ycv", [128, 8, 512])
            sg = [sb(f"sg{i}", [128, 512]) for i in range(2)]
            YCt = sb("YCt", [128, 8, 512], BF16)
            GAt = sb("GAt", [128, 8, 512], BF16)
            bg = sb("bg", [128, 16])
            scw = sb("scw", [128, 3, 8])
            ident = sb("ident", [128, 128], BF16)

            GROUPS = [("q", 0, 0), ("v", 2048, 0), ("q", 512, 1), ("cc", 4096, 0), ("k", 1024, 0), ("v", 2560, 1),
                      ("cx", 5120, 0), ("cc", 4608, 1), ("k", 1536, 1), ("ga", 6144, 0), ("cx", 5632, 1), ("ga", 6656, 1),
                      ("cb", 3072, 0), ("gc", 7168, 0), ("cb", 3584, 1), ("gc", 7680, 1)]
            consts = [(cosT, cos_d), (sinT, sin_d), (g1bc, g1bc_d), (qg, qg_d), (kg, kg_d),
                      (bg, bg_d), (scw, scw_d)]
            c_ops = []
            for dst, src in consts:
                c_ops.append(SP.add(lambda e, dst=dst, src=src: e.dma_start(out=dst[:], in_=src), ev=ev_const))
            c_last = c_ops[-1]
            id_op = POOL.add(lambda e: e.dma_start(out=ident[:], in_=ident_d), ev=ev_win)
            win_last = id_op
            ev_wc = [Ev(nc, es, f"A_wc{i}", 16) for i in range(16)]
            win_ops = {}

            def pool_cast(gi_):
                col_ = GROUPS[gi_][1]
                win_ops[gi_] = POOL.add(
                    lambda e: e.dma_start(out=Wb_in[:, col_:col_ + 512], in_=w_in[:, col_:col_ + 512]), ev=ev_wc[gi_])
            for gi_ in range(3):
                pool_cast(gi_)
            m0 = DVE.add(lambda e: e.memset(neghalf[:], -0.5))
            m1 = DVE.add(lambda e: e.memset(ubuf[:], 0.0))

            x_v = x.rearrange("(t s p) d -> t p s d", s=4, p=128)
            Wv = Wb_in.rearrange("(kc p) n -> p kc n", p=128)

            NG = len(GROUPS)

            state = dict(xload={}, prep_hb=None, rd={}, pe_unit={}, wload={}, last_unit_of_group={},
                         hT_ready={}, trh_evac=[], trq_evac=None, pe_trq={}, qk_cnt=0, v_cnt=0,
                         dve_red={}, dve_fin={}, st_qk=[None, None], st_v=[None] * 8, st_yc=None, st_ga=None,
                         halo=[m1] * 8, pool_yc=[None] * 8, sg_pool=[None, None], gc_cnt=0,
                         act_sq4={}, dve_hb_last={}, pe_trh_last={}, pending_stores=[])

            def sp_xload(t):
                slot = t % 2
                waits = []
                if t >= 2:
                    waits += [state["dve_hb_last"][t - 2], state["act_sq4"][t - 2]]
                state["xload"][t] = SP.add(lambda e: e.dma_start(out=xin[slot][:], in_=x_v[t]), waits, ev=ev_ldx[slot])

            def act_prep(t):
                slot = t % 2
                op = None
                for s in range(4):
                    op = ACT.add(lambda e, s=s: e.activation(out=junk[:], in_=xin[slot][:, s, :], func=AF.Square,
                                                            accum_out=ss4[:, s:s + 1]),
                                 [state["xload"][t]])
                state["act_sq4"][t] = op

            def dve_prep(t):
                slot = t % 2
                dv = DVE.add(lambda e: e.tensor_scalar(out=var4[:], in0=ss4[:], scalar1=1.0 / D, scalar2=EPS,
                                                       op0=ALU.mult, op1=ALU.add), [state["act_sq4"][t]])
                pw = POOL.add(lambda e: e.tensor_tensor(out=rstd4[:], in0=var4[:], in1=neghalf[:, 0:4], op=ALU.pow),
                              [dv, m0])
                ops = []
                for s in range(4):
                    w = [pw, c_last]
                    if t >= 1:
                        w.append(state["pe_trh_last"][t - 1])
                    ops.append(DVE.add(lambda e, s=s: e.scalar_tensor_tensor(
                        out=hb[:, s, :], in0=xin[slot][:, s, :], scalar=rstd4[:, s:s + 1], in1=g1bc[:],
                        op0=ALU.mult, op1=ALU.mult), w))
                state["hb_ops", t] = ops
                state["dve_hb_last"][t] = ops[-1]

            def pe_trh(t, s):
                w = [state["hb_ops", t][s], win_last]
                ev_list = state["trh_evac"]
                if len(ev_list) >= 2:
                    w.append(ev_list[-2])
                bk = 5 + (s % 2)

                def fn(e):
                    ins = None
                    for kc in range(8):
                        ins = e.transpose(out=bank_bf(bk)[:, kc * 128:(kc + 1) * 128],
                                          in_=hb[:, s, kc * 128:(kc + 1) * 128], identity=ident[:])
                    return ins
                op = PE.add(fn, w)
                state["pe_trh", t, s] = op
                if s == 3:
                    state["pe_trh_last"][t] = op

            def act_trh_evac(t, s):
                bk = 5 + (s % 2)
                op = ACT.add(lambda e: e.copy(out=hT[t % 2][:, :, s * 128:(s + 1) * 128],
                                              in_=bank_bf(bk).rearrange("p (k t) -> p k t", k=8)),
                             [state["pe_trh", t, s]])
                state["trh_evac"].append(op)
                if s == 3:
                    state["hT_ready"][t] = op

            def sp_wload(t, gi):
                gg = t * NG + gi
                slot = gg % 3
                col = GROUPS[gi][1]
                waits = [win_ops[gi]]
                if gg >= 3:
                    waits.append(state["last_unit_of_group"][gg - 3])
                state["wload"][gg] = SP.add(lambda e: e.dma_start(out=Wb[slot][:], in_=Wv[:, :, col:col + 512]),
                                            waits, ev=ev_ldw[slot])

            def pe_unit(t, gi, s):
                gg = t * NG + gi
                gu = gg * 4 + s
                kind = GROUPS[gi][0]
                slot = gg % 3
                bk = gu % 5
                waits = [state["wload"][gg], state["hT_ready"][t]]
                if gu >= 5:
                    waits += state["rd"][gu - 5]
                tok_major = kind in ("q", "k", "v")

                def fn(e):
                    ins = None
                    for kc in range(8):
                        if tok_major:
                            lhsT = hT[t % 2][:, kc, s * 128:(s + 1) * 128]
                            rhs = Wb[slot][:, kc, :]
                        else:
                            lhsT = Wb[slot][:, kc, s * 128:(s + 1) * 128]
                            rhs = hT[t % 2][:, kc, :]
                        ins = e.matmul(pall[:, bk, :], lhsT=lhsT, rhs=rhs, start=(kc == 0), stop=(kc == 7))
                    return ins
                op = PE.add(fn, waits)
                state["pe_unit"][gu] = op
                if s == 3:
                    state["last_unit_of_group"][gg] = op
                return op

            def consume_qk(t, gi, s, mm):
                gu = (t * NG + gi) * 4 + s
                bk = gu % 5
                n = state["qk_cnt"]
                state["qk_cnt"] += 1
                wbi = n % 2
                gsel = qg if GROUPS[gi][0] == "q" else kg
                cidx = t * 4 + s
                w = [mm]
                if n >= 2:
                    w.append(state["dve_red"][n - 2])
                a1 = ACT.add(lambda e: e.activation(out=sq[wbi][:], in_=pall[:, bk, :], func=AF.Square), w)
                red = DVE.add(lambda e: e.tensor_reduce(out=ssq[wbi][:], in_=sq[wbi][:].rearrange("p (g d) -> p g d", g=8),
                                                        axis=AX.X, op=ALU.add), [a1])
                state["dve_red"][n] = red
                dv = DVE.add(lambda e: e.tensor_scalar(out=var8[wbi][:], in0=ssq[wbi][:], scalar1=1.0 / 64, scalar2=EPS,
                                                       op0=ALU.mult, op1=ALU.add), hard=[red])
                pw = POOL.add(lambda e: e.tensor_tensor(out=rs8[wbi][:], in0=var8[wbi][:], in1=neghalf[:], op=ALU.pow),
                              [dv, m0])
                v3 = lambda ap: ap.rearrange("p (g d) -> p g d", g=8)
                v4 = lambda ap: ap.rearrange("p (g h d) -> p g h d", g=8, h=2)
                d1 = DVE.add(lambda e: e.tensor_tensor(out=v3(t0[wbi][:]), in0=v3(pall[:, bk, :]),
                                                       in1=gsel[:].unsqueeze(1).broadcast_to([128, 8, 64]), op=ALU.mult),
                             [mm, c_last, a1])
                state["rd"][gu] = [a1, d1]
                cb4 = cosT[:, cidx, :].unsqueeze(1).unsqueeze(1).broadcast_to([128, 8, 2, 32])
                sb4 = sinT[:, cidx, :].unsqueeze(1).unsqueeze(1).broadcast_to([128, 8, 2, 32])
                DVE.add(lambda e: e.tensor_tensor(out=v4(tA[wbi][:]), in0=v4(t0[wbi][:]), in1=cb4, op=ALU.mult), sig=False)
                DVE.add(lambda e: e.tensor_tensor(out=v4(tB[wbi][:]), in0=v4(t0[wbi][:]), in1=sb4, op=ALU.mult), sig=False)
                DVE.add(lambda e: e.tensor_tensor(out=v4(oo[wbi][:])[:, :, 0, :], in0=v4(tA[wbi][:])[:, :, 0, :],
                                                  in1=v4(tB[wbi][:])[:, :, 1, :], op=ALU.subtract), sig=False)
                DVE.add(lambda e: e.tensor_tensor(out=v4(oo[wbi][:])[:, :, 1, :], in0=v4(tA[wbi][:])[:, :, 1, :],
                                                  in1=v4(tB[wbi][:])[:, :, 0, :], op=ALU.add), sig=False)
                w = [pw]
                if n >= 8:
                    w.append(state["pe_trq"][n - 8])
                fin = DVE.add(lambda e: e.tensor_tensor(out=v3(qn[n % 8][:]), in0=v3(oo[wbi][:]),
                                                        in1=rs8[wbi][:].unsqueeze(2).broadcast_to([128, 8, 64]),
                                                        op=ALU.mult), w)
                state["dve_fin"][n] = fin
                return n

            def pe_trq(n):
                wbi = n % 8
                w = [state["dve_fin"][n]]
                if state["trq_evac"] is not None:
                    w.append(state["trq_evac"])

                def fn(e):
                    ins = None
                    for hh in range(4):
                        ins = e.transpose(out=bank_bf(7)[:, hh * 128:(hh + 1) * 128],
                                          in_=qn[wbi][:, hh * 128:(hh + 1) * 128], identity=ident[:])
                    return ins
                state["pe_trq"][n] = PE.add(fn, w)

            def act_trq_evac(n, t, gi, s):
                slot = state["qkg", t, gi] % 2
                w = [state["pe_trq"][n]]
                if s == 0 and state["st_qk"][slot] is not None:
                    w.append(state["st_qk"][slot])
                op = ACT.add(lambda e: e.copy(out=QTt[slot][:, :, s * 128:(s + 1) * 128],
                                              in_=bank_bf(7)[:, 0:512].rearrange("p (h t) -> p h t", h=4)), w)
                state["trq_evac"] = op
                if s == 3:
                    dst = QT if GROUPS[gi][0] == "q" else KT
                    h0 = GROUPS[gi][2] * 4
                    state["pending_stores"].append((t * NG + gi, lambda: _store_qk(slot, dst, h0, t, op)))

            def _store_qk(slot, dst, h0, t, op):
                state["st_qk"][slot] = SP.add(
                    lambda e: e.dma_start(out=dst[h0:h0 + 4, :, t * 512:(t + 1) * 512].rearrange("h p t -> p h t"),
                                          in_=QTt[slot][:]), [op], ev=ev_stqk[slot])

            def consume_v(t, gi, s, mm):
                gu = (t * NG + gi) * 4 + s
                bk = gu % 5
                n = state["v_cnt"]
                state["v_cnt"] += 1
                slot = n % 8
                w = [mm]
                if state["st_v"][slot] is not None:
                    w.append(state["st_v"][slot])
                a = ACT.add(lambda e: e.copy(out=vt[slot][:], in_=pall[:, bk, :]), w)
                state["rd"][gu] = [a]
                h0 = GROUPS[gi][2] * 4
                c = t * 4 + s

                def mk():
                    state["st_v"][slot] = SP.add(
                        lambda e: e.dma_start(out=Vs[h0:h0 + 4, :, c, :].rearrange("h p e -> p h e"),
                                              in_=vt[slot][:].rearrange("p (h e) -> p h e", h=4)), [a], ev=ev_stv[slot])
                state["pending_stores"].append((t * NG + gi, mk))

            def consume_conv(t, gi, s, mm):
                gu = (t * NG + gi) * 4 + s
                bk = gu % 5
                kind = GROUPS[gi][0]
                c = GROUPS[gi][2] * 4 + s
                if kind == "cc":
                    a = ACT.add(lambda e: e.copy(out=ubuf[:, c, 2:514], in_=pall[:, bk, :]), [mm, state["halo"][c]])
                    state["cc", c] = a
                    state["rd"][gu] = [a]
                elif kind == "cx":
                    d = DVE.add(lambda e: e.tensor_tensor(out=ubuf[:, c, 2:514], in0=pall[:, bk, :], in1=ubuf[:, c, 2:514],
                                                          op=ALU.mult), [mm, state["cc", c]])
                    state["rd"][gu] = [d]
                    w = [c_last]
                    if state["pool_yc"][c] is not None:
                        w.append(state["pool_yc"][c])
                    DVE.add(lambda e: e.tensor_scalar(out=ycv[:, c, :], in0=ubuf[:, c, 2:514], scalar1=scw[:, 2, c:c + 1],
                                                      scalar2=None, op0=ALU.mult), w, sig=False)
                    DVE.add(lambda e: e.scalar_tensor_tensor(out=ycv[:, c, :], in0=ubuf[:, c, 1:513], scalar=scw[:, 1, c:c + 1],
                                                             in1=ycv[:, c, :], op0=ALU.mult, op1=ALU.add), sig=False)
                    DVE.add(lambda e: e.scalar_tensor_tensor(out=ycv[:, c, :], in0=ubuf[:, c, 0:512], scalar=scw[:, 0, c:c + 1],
                                                             in1=ycv[:, c, :], op0=ALU.mult, op1=ALU.add), sig=False)
                    state["halo"][c] = DVE.add(lambda e: e.tensor_copy(out=ubuf[:, c, 0:2], in_=ubuf[:, c, 512:514]))
                elif kind == "cb":
                    d = DVE.add(lambda e: e.tensor_tensor(out=ycv[:, c, :], in0=pall[:, bk, :], in1=ycv[:, c, :],
                                                          op=ALU.mult), [mm])
                    state["cb", c] = d
                    state["rd"][gu] = [d]
                elif kind == "gc":
                    n = state["gc_cnt"]
                    state["gc_cnt"] += 1
                    si = n % 2
                    w = [mm, c_last]
                    if state["sg_pool"][si] is not None:
                        w.append(state["sg_pool"][si])
                    a = ACT.add(lambda e: e.activation(out=sg[si][:], in_=pall[:, bk, :], func=AF.Sigmoid,
                                                       bias=bg[:, 8 + c:9 + c]), w)
                    state["rd"][gu] = [a]
                    w = [a, state["cb", c]]
                    if c == 0 and state["st_yc"] is not None:
                        w.append(state["st_yc"])
                    p = POOL.add(lambda e: e.tensor_tensor(out=YCt[:, c, :], in0=sg[si][:], in1=ycv[:, c, :], op=ALU.mult), w)
                    state["sg_pool"][si] = p
                    state["pool_yc"][c] = p
                    if c == 7:
                        def mk():
                            state["st_yc"] = SP.add(
                                lambda e: e.dma_start(out=YC[:, :, t * 512:(t + 1) * 512].rearrange("c p t -> p c t"),
                                                      in_=YCt[:]), [p], ev=ev_styc)
                        state["pending_stores"].append((t * NG + gi, mk))
                elif kind == "ga":
                    w = [mm, c_last]
                    if c == 0 and state["st_ga"] is not None:
                        w.append(state["st_ga"])
                    a = ACT.add(lambda e: e.activation(out=GAt[:, c, :], in_=pall[:, bk, :], func=AF.Sigmoid,
                                                       bias=bg[:, c:c + 1]), w)
                    state["rd"][gu] = [a]
                    if c == 7:
                        def mk():
                            state["st_ga"] = SP.add(
                                lambda e: e.dma_start(out=GA[:, :, t * 512:(t + 1) * 512].rearrange("c p t -> p c t"),
                                                      in_=GAt[:]), [a], ev=ev_stga)
                        state["pending_stores"].append((t * NG + gi, mk))

            sp_xload(0)
            if NT > 1:
                sp_xload(1)
            act_prep(0)
            dve_prep(0)
            for s in range(4):
                pe_trh(0, s)
                act_trh_evac(0, s)
            for gi in range(min(3, NG)):
                sp_wload(0, gi)

            qk_pending = []
            for t in range(NT):
                for gi in range(NG):
                    gg = t * NG + gi
                    if GROUPS[gi][0] in ("q", "k"):
                        state["qkg", t, gi] = state.get("qkg_cnt", 0)
                        state["qkg_cnt"] = state.get("qkg_cnt", 0) + 1
                    nxt = gg + 3
                    if gi == 0 and t + 2 < NT and t >= 0:
                        pass
                    for s in range(4):
                        u = gi * 4 + s
                        if u == 38 and t + 1 < NT:
                            act_prep(t + 1)
                            dve_prep(t + 1)
                        if u == 56 and t + 1 < NT:
                            for s2 in range(4):
                                pe_trh(t + 1, s2)
                                act_trh_evac(t + 1, s2)
                        mm = pe_unit(t, gi, s)
                        kind = GROUPS[gi][0]
                        if kind in ("q", "k"):
                            n = consume_qk(t, gi, s, mm)
                            qk_pending.append((n, t, gi, s, t * 64 + u))
                        elif kind == "v":
                            consume_v(t, gi, s, mm)
                        else:
                            consume_conv(t, gi, s, mm)
                        cur = t * 64 + u
                        while qk_pending and qk_pending[0][4] + 6 <= cur:
                            n0, t0_, gi0, s0, _ = qk_pending.pop(0)
                            pe_trq(n0)
                            act_trq_evac(n0, t0_, gi0, s0)
                    if t == 0 and gi + 3 < NG:
                        pool_cast(gi + 3)
                    if nxt < NT * NG:
                        sp_wload(nxt // NG, nxt % NG)
                    if gi == 8 and t + 2 < NT:
                        sp_xload(t + 2)
                    keep = []
                    for (g0, mk) in state["pending_stores"]:
                        if g0 + 2 <= gg:
                            mk()
                        else:
                            keep.append((g0, mk))
                    state["pending_stores"] = keep
            while qk_pending:
                n0, t0_, gi0, s0, _ = qk_pending[0]
                if n0 not in state["pe_trq"]:
                    pe_trq(n0)
                qk_pending.pop(0)
                act_trq_evac(n0, t0_, gi0, s0)
            for (g0, mk) in state["pending_stores"]:
                mk()
            state["pending_stores"] = []
            fin_waits = [o for o in state["st_qk"] + state["st_v"] + [state["st_yc"], state["st_ga"]] if o is not None]
            SP.add(lambda e: e.nop(), fin_waits, ev=None)
            run_block(nc, st)

    def phase_B():
        NQ = S // 512
        with ExitStack() as ps:
            def sb(name, shape, dt=F32):
                return ps.enter_context(nc.sbuf_tensor("B_" + name, list(shape), dt))

            st = mk_streams(nc, es, "B")
            PE, ACT, DVE, POOL, SP = st["pe"], st["act"], st["dve"], st["pool"], st["sp"]
            ev_const = Ev(nc, es, "B_const", 16)
            ev_c2 = Ev(nc, es, "B_c2", 16)
            ev_ldq = [Ev(nc, es, f"B_ldq{i}", 16) for i in range(2)]
            ev_ldk = [Ev(nc, es, f"B_ldk{i}", 16) for i in range(2)]
            ev_ldv = [Ev(nc, es, f"B_ldv{i}", 16) for i in range(2)]
            ev_ldga = [Ev(nc, es, f"B_ldga{i}", 16) for i in range(2)]
            ev_ldyc = [Ev(nc, es, f"B_ldyc{i}", 16) for i in range(2)]
            ev_stmt = [Ev(nc, es, f"B_stmt{i}", 16) for i in range(2)]

            QTh = [sb(f"QTh{i}", [128, S], BF16) for i in range(2)]
            KTh = [sb(f"KTh{i}", [128, S], BF16) for i in range(2)]
            Vh = [sb(f"Vh{i}", [128, NCH, 130], BF16) for i in range(2)]
            NP = 4
            P = [sb(f"P{i}", [128, 2, 512], BF16) for i in range(NP)]
            mneg = sb("mneg", [128, 128], BF16)
            ident = sb("ident", [128, 128], BF16)
            GAq = [sb(f"GAq{i}", [128, 512], BF16) for i in range(2)]
            YCq = [sb(f"YCq{i}", [128, 512], BF16) for i in range(2)]
            lam4 = sb("lam4", [128, 4, 64])
            ljunk = sb("ljunk", [128, 64])
            s12 = sb("s12", [128, 2])
            e12 = sb("e12", [128, 2])
            nlam = sb("nlam", [128, 1])
            subg = sb("subg", [128, 128])
            sg08 = sb("sg08", [128, 128])
            neghalf = sb("neghalf", [128, 4])
            rr9 = sb("rr9", [128, 3, 3])
            oraw = sb("oraw", [128, 3, 387])
            oa = sb("oa", [128, 4, 128])
            ob = sb("ob", [128, 4, 128])
            od = sb("od", [128, 4, 128])
            djunk = sb("djunk", [128, 128])
            ssd = sb("ssd", [128, 4])
            vard = sb("vard", [128, 4])
            rsd = sb("rsd", [128, 4])
            ofin = sb("ofin", [128, 4, 128], BF16)
            mtmp = sb("mtmp", [128, 512])
            mt = [sb(f"mt{i}", [128, 512], BF16) for i in range(2)]

            c1 = SP.add(lambda e: e.dma_start(out=lam4[:], in_=lam_d), ev=ev_const)
            c2 = SP.add(lambda e: e.dma_start(out=subg[:], in_=sub_d), ev=ev_const)
            POOL.add(lambda e: e.dma_start(out=mneg[:], in_=mneg_d), ev=ev_c2)
            p2 = POOL.add(lambda e: e.dma_start(out=ident[:], in_=ident_d), ev=ev_c2)
            for i in range(8):
                POOL.add(lambda e, i=i: e.dma_start(out=Wb_up[i * 128:(i + 1) * 128, :], in_=w_up[i * 128:(i + 1) * 128, :]),
                         ev=ev_wrest)
            for i in range(DFF // 128 // 2):
                POOL.add(lambda e, i=i: e.dma_start(out=Wb_dn[i * 256:(i + 1) * 256, :], in_=w_dn[i * 256:(i + 1) * 256, :]),
                         ev=ev_wrest)
            POOL.add(lambda e: e.dma_start(out=Wb_out[:, :], in_=w_out[:, :]), ev=ev_wrest)
            mv = []
            for i in range(2):
                mv.append(DVE.add(lambda e, i=i: e.memset(Vh[i][:, :, 128:130], 1.0)))
            mnh = DVE.add(lambda e: e.memset(neghalf[:], -0.5))
            for k in range(2):
                DVE.add(lambda e, k=k: e.scalar_tensor_tensor(out=ljunk[:], in0=lam4[:, 2 * k, :], scalar=1.0,
                                                            in1=lam4[:, 2 * k + 1, :], op0=ALU.mult, op1=ALU.mult,
                                                            accum_out=s12[:, k:k + 1]), [c1, c2], sig=False)
            dsg = DVE.add(lambda e: e.tensor_scalar(out=sg08[:], in0=subg[:], scalar1=1.0 - LAMBDA_INIT, scalar2=None,
                                                    op0=ALU.mult), [c1, c2])
            aexp = ACT.add(lambda e: e.activation(out=e12[:], in_=s12[:], func=AF.Exp), [dsg])
            dl0 = DVE.add(lambda e: e.tensor_tensor(out=nlam[:], in0=e12[:, 1:2], in1=e12[:, 0:1], op=ALU.subtract), [aexp])
            dlam = DVE.add(lambda e: e.tensor_scalar(out=nlam[:], in0=nlam[:], scalar1=-LAMBDA_INIT, scalar2=None,
                                                     op0=ALU.add), hard=[dl0])

            def acc(a):
                o = (a % 3) * 129
                return pall[:, 4 + a // 3, o:o + 129]

            chunks = []
            groups = []
            for h in range(NH):
                for qb in range(NQ):
                    g = len(groups)
                    first = len(chunks)
                    for kc in range(4 * qb + 4):
                        chunks.append((h, qb, kc, kc - 4 * qb, g))
                    groups.append((h, qb, first, len(chunks) - 1))
            NCk = len(chunks)
            NG = len(groups)

            S_ = dict(qk={}, exp={}, pready={}, pv={}, head_ld={}, last_pv_head={}, evac1={}, ofin={}, trg={},
                      merge={}, ldga={}, st={}, mask={})

            def sp_head_load(h):
                hp = h % 2
                w = []
                if h >= 2:
                    w.append(S_["last_pv_head"][h - 2])
                a = SP.add(lambda e: e.dma_start(out=QTh[hp][:], in_=QT[h]), w, ev=ev_ldq[hp])
                b = SP.add(lambda e: e.dma_start(out=KTh[hp][:], in_=KT[h]), w, ev=ev_ldk[hp])
                c = SP.add(lambda e: e.dma_start(out=Vh[hp][:, :, 0:128], in_=Vs[h]), w + mv, ev=ev_ldv[hp])
                S_["head_ld"][h] = [a, b, c]

            def sp_group_load(g):
                h, qb, _, _ = groups[g]
                sl = g % 2
                w = []
                if g >= 2:
                    w.append(S_["merge"][g - 2])
                a = SP.add(lambda e: e.dma_start(out=GAq[sl][:], in_=GA[h, :, qb * 512:(qb + 1) * 512]), w, ev=ev_ldga[sl])
                b = SP.add(lambda e: e.dma_start(out=YCq[sl][:], in_=YC[h, :, qb * 512:(qb + 1) * 512]), w, ev=ev_ldyc[sl])
                S_["ldga"][g] = [a, b]

            def sp_group_store(g):
                h, qb, _, _ = groups[g]
                sl = g % 2
                S_["st"][g] = SP.add(lambda e: e.dma_start(out=MT[h, :, qb * 512:(qb + 1) * 512], in_=mt[sl][:]),
                                     [S_["merge"][g]], ev=ev_stmt[sl])

            def pe_qk(i):
                h, qb, kc, j, g = chunks[i]
                hp = h % 2
                sbuf_i = i % 2
                c0 = 128 * max(j, 0)
                w = list(S_["head_ld"][h][0:2])
                if i >= 2:
                    w.append(S_["exp"][i - 2])

                def fn(e):
                    ins = None
                    for m in range(2):
                        ins = e.matmul(pall[:, sbuf_i * 2 + m, c0:512],
                                       lhsT=KTh[hp][64 * m:64 * m + 64, kc * 128:(kc + 1) * 128],
                                       rhs=QTh[hp][64 * m:64 * m + 64, qb * 512 + c0:(qb + 1) * 512],
                                       start=True, stop=True)
                    if j >= 0:
                        for m in range(2):
                            ins = e.matmul(pall[:, sbuf_i * 2 + m, c0:c0 + 128], lhsT=ident[:], rhs=mneg[:],
                                           start=False, stop=True, skip_group_check=True)
                    return ins
                if j >= 0:
                    w.append(p2)
                S_["qk"][i] = PE.add(fn, w)

            def act_exp(i):
                h, qb, kc, j, g = chunks[i]
                sbuf_i = i % 2
                pb = i % NP
                c0 = 128 * max(j, 0)
                w = [S_["qk"][i]]
                if i >= NP:
                    w.append(S_["pv"][i - NP])
                op = ACT.add(lambda e: e.activation(out=P[pb][:, :, c0:512], in_=pall[:, sbuf_i * 2:sbuf_i * 2 + 2, c0:512],
                                                    func=AF.Exp, scale=0.125), w)
                S_["exp"][i] = op
                S_["pready"][i] = op

            def pe_pv(i):
                h, qb, kc, j, g = chunks[i]
                hp = h % 2
                pb = i % NP
                w = [S_["pready"][i], S_["head_ld"][h][2]]
                if kc == 0 and g >= 1:
                    w.append(S_["evac1"][g - 1])

                def fn(e):
                    ins = None
                    for t in range(max(j, 0), 4):
                        for m in range(2):
                            a = t * 2 + m
                            ins = e.matmul(acc(a), lhsT=P[pb][:, m, t * 128:(t + 1) * 128], rhs=Vh[hp][:, kc, 0:129],
                                           start=(kc == 0 and a % 3 == 0), stop=(kc == 4 * qb + t),
                                           skip_group_check=True)
                    return ins
                op = PE.add(fn, w)
                S_["pv"][i] = op
                S_["last_pv_head"][h] = op

            def dve_evac(g):
                h, qb, first, last = groups[g]
                w = [S_["pv"][last], dlam, mnh]
                cp = DVE.add(lambda e: e.tensor_copy(out=oraw[:], in_=pall[:, 4:7, 0:387]), w)
                S_["evac1"][g] = cp
                rc = DVE.add(lambda e: e.reciprocal(out=rr9[:], in_=oraw[:, :, 128:387:129]))

                def accs(a):
                    o = (a % 3) * 129
                    return oraw[:, a // 3, o:o + 128]
                op = None
                for t in range(4):
                    a0, a1 = 2 * t, 2 * t + 1
                    DVE.add(lambda e, t=t, a0=a0: e.tensor_scalar(out=oa[:, t, :], in0=accs(a0),
                                                                  scalar1=rr9[:, a0 // 3, a0 % 3:a0 % 3 + 1], scalar2=None,
                                                                  op0=ALU.mult), sig=False, hard=[rc] if t == 0 else [])
                    op = DVE.add(lambda e, t=t, a1=a1: e.tensor_scalar(out=ob[:, t, :], in0=accs(a1),
                                                                       scalar1=rr9[:, a1 // 3, a1 % 3:a1 % 3 + 1],
                                                                       scalar2=None, op0=ALU.mult), sig=(t == 3))
                odl = None
                for t in range(4):
                    odl = DVE.add(lambda e, t=t: e.scalar_tensor_tensor(out=od[:, t, :], in0=ob[:, t, :], scalar=nlam[:, 0:1],
                                                                       in1=oa[:, t, :], op0=ALU.mult, op1=ALU.add),
                                  hard=[op] if t == 0 else [])
                ssl = None
                for t in range(4):
                    ssl = DVE.add(lambda e, t=t: e.scalar_tensor_tensor(out=djunk[:], in0=od[:, t, :], scalar=1.0,
                                                                       in1=od[:, t, :], op0=ALU.mult, op1=ALU.mult,
                                                                       accum_out=ssd[:, t:t + 1]),
                                  hard=[odl] if t == 0 else [])
                dv = DVE.add(lambda e: e.tensor_scalar(out=vard[:], in0=ssd[:], scalar1=1.0 / 128, scalar2=EPS,
                                                       op0=ALU.mult, op1=ALU.add), hard=[ssl])
                pw = POOL.add(lambda e: e.tensor_tensor(out=rsd[:], in0=vard[:], in1=neghalf[:], op=ALU.pow), [dv, mnh])
                w = [pw]
                if g >= 1:
                    w.append(S_["trg"][g - 1])
                op = None
                for t in range(4):
                    op = DVE.add(lambda e, t=t: e.scalar_tensor_tensor(out=ofin[:, t, :], in0=od[:, t, :],
                                                                      scalar=rsd[:, t:t + 1], in1=sg08[:],
                                                                      op0=ALU.mult, op1=ALU.mult), w, sig=(t == 3))
                S_["ofin"][g] = op

            def pe_trg(g):
                w = [S_["ofin"][g], p2]
                if g >= 1:
                    w.append(S_["merge"][g - 1])

                def fn(e):
                    ins = None
                    for t in range(4):
                        ins = e.transpose(out=bank_bf(7)[:, t * 128:(t + 1) * 128], in_=ofin[:, t, :], identity=ident[:])
                    return ins
                S_["trg"][g] = PE.add(fn, w)

            def dve_merge(g):
                sl = g % 2
                w = [S_["trg"][g]] + S_["ldga"][g]
                if g >= 2:
                    w.append(S_["st"][g - 2])
                DVE.add(lambda e: e.tensor_tensor(out=mtmp[:], in0=bank_bf(7)[:, 0:512], in1=GAq[sl][:], op=ALU.mult), w,
                        sig=False)
                S_["merge"][g] = DVE.add(lambda e: e.tensor_tensor(out=mt[sl][:], in0=mtmp[:], in1=YCq[sl][:], op=ALU.add))

            sp_head_load(0)
            sp_group_load(0)
            if NG > 1:
                sp_group_load(1)
            pe_qk(0)
            act_exp(0)
            if NCk > 1:
                pe_qk(1)
                act_exp(1)
            for i in range(NCk):
                h, qb, kc, j, g = chunks[i]
                _, _, first, last = groups[g]
                if i == first and qb == 0 and h + 1 < NH:
                    sp_head_load(h + 1)
                if i + 2 < NCk:
                    pe_qk(i + 2)
                    act_exp(i + 2)
                pe_pv(i)
                if g >= 1 and i == min(first + 9, last):
                    pe_trg(g - 1)
                    dve_merge(g - 1)
                    sp_group_store(g - 1)
                    if g + 1 < NG:
                        sp_group_load(g + 1)
                if i == last:
                    dve_evac(g)
            pe_trg(NG - 1)
            dve_merge(NG - 1)
            sp_group_store(NG - 1)
            SP.add(lambda e: e.nop(), [S_["st"][g] for g in range(max(0, NG - 2), NG)], ev=None)
            run_block(nc, st)

    def phase_C():
        with ExitStack() as ps:
            def sb(name, shape, dt=F32):
                return ps.enter_context(nc.sbuf_tensor("C_" + name, list(shape), dt))

            st = mk_streams(nc, es, "C")
            PE, ACT, DVE, POOL, SP = st["pe"], st["act"], st["dve"], st["pool"], st["sp"]
            ev_const = Ev(nc, es, "C_const", 16)
            ev_c2 = Ev(nc, es, "C_c2", 16)
            ev_ldx = [Ev(nc, es, f"C_ldx{i}", 16) for i in range(2)]
            ev_ldm = [Ev(nc, es, f"C_ldm{i}", 16) for i in range(1)]
            ev_ldu = [Ev(nc, es, f"C_ldu{i}", 16) for i in range(3)]
            ev_ldd = [Ev(nc, es, f"C_ldd{i}", 16) for i in range(4)]
            ev_sto = [Ev(nc, es, f"C_sto{i}", 16) for i in range(2)]

            x1 = [sb(f"x1{i}", [128, 4, D]) for i in range(2)]
            MTt = [sb(f"MTt{i}", [128, 8, 512], BF16) for i in range(1)]
            Wo = sb("Wo", [128, 8, D], BF16)
            Wu = [sb(f"Wu{i}", [128, 8, 2, 256], BF16) for i in range(3)]
            Wd = [sb(f"Wd{i}", [128, 11, 512], BF16) for i in range(4)]
            hb = sb("hb", [128, 4, D], BF16)
            hT = sb("hT", [128, 8, 512], BF16)
            actT = sb("actT", [128, 22, 512], BF16)
            g2bc = sb("g2bc", [128, D])
            fcw = sb("fcw", [128, 3, 44])
            fcb = sb("fcb", [128, 44])
            ident = sb("ident", [128, 128], BF16)
            neghalf = sb("neghalf", [128, 4])
            ss4 = sb("ss4", [128, 4])
            var4 = sb("var4", [128, 4])
            rstd4 = sb("rstd4", [128, 4])
            junk = sb("junk", [128, D], BF16)
            halo = sb("halo", [128, 44, 2])
            NU = 6
            NY = 3
            ub = [sb(f"ub{i}", [128, 514]) for i in range(NU)]
            ya = [sb(f"ya{i}", [128, 512]) for i in range(NY)]
            yb = [sb(f"yb{i}", [128, 512]) for i in range(NY)]
            sa = [sb(f"sa{i}", [128, 512]) for i in range(NY)]

            NTl = S // 512
            c_list = []
            for dst, src in [(g2bc, g2bc_d), (fcw, fcw_d), (fcb, fcb_d)]:
                c_list.append(SP.add(lambda e, dst=dst, src=src: e.dma_start(out=dst[:], in_=src), ev=ev_const))
            p_id = POOL.add(lambda e: e.dma_start(out=ident[:], in_=ident_d), ev=ev_c2)
            class _W:
                pass
            wrest = Op(None, [], ev_wrest, None)
            wrest.val = ev_wrest.n
            wo_ld = SP.add(lambda e: e.dma_start(out=Wo[:], in_=Wb_out.rearrange("(kc p) n -> p kc n", p=128)), [wrest],
                           ev=ev_const)
            c_list.append(wo_ld)
            mh = DVE.add(lambda e: e.memset(halo[:], 0.0))
            mnh = DVE.add(lambda e: e.memset(neghalf[:], -0.5))

            x_v = x.rearrange("(t s p) d -> t p s d", s=4, p=128)
            o_v = out.rearrange("(t s p) d -> t p s d", s=4, p=128)
            Wuv = Wb_up.rearrange("(kc p) (ab n) -> p kc ab n", p=128, ab=2)
            Wdv = Wb_dn.rearrange("(c p) n -> p c n", p=128)

            C_ = dict(xld={}, mld={}, sto=[None, None], uld={}, dld={}, rd={}, unit_cnt=0, last_pe_tile={},
                      trh_evac=[], pe_trh={}, x1add={}, hbops={}, hT_ready={}, last_read_hT={}, act_ready={},
                      ug_last={}, dg_last={}, halo_rd=[mh] * 44, ub_free=[None] * NU, ucnt=0, pair_cnt=0,
                      ya_free=[None] * 3, yb_free=[None] * 3, sa_free=[None] * 3, lag=None, fin_add={}, down_last={},
                      wo_done={}, act_sq={})

            def sp_xload(t):
                sl = t % 2
                w = []
                if t >= 2:
                    w.append(C_["sto"][sl])
                C_["xld"][t] = SP.add(lambda e: e.dma_start(out=x1[sl][:], in_=x_v[t]), w, ev=ev_ldx[sl])

            def sp_mload(t):
                w = []
                if t >= 1:
                    w.append(C_["wo_done"][t - 1])
                C_["mld"][t] = SP.add(
                    lambda e: e.dma_start(out=MTt[0][:], in_=MT[:, :, t * 512:(t + 1) * 512].rearrange("c p t -> p c t")),
                    w, ev=ev_ldm[0])

            def sp_uload(t, gi):
                gg = t * 11 + gi
                sl = gg % 3
                w = [wrest]
                if gg >= 3:
                    w.append(C_["ug_last"][gg - 3])
                for ab in range(2):
                    C_["uld"][gg] = SP.add(
                        lambda e, ab=ab: e.dma_start(out=Wu[sl][:, :, ab, :], in_=Wuv[:, :, ab, gi * 256:(gi + 1) * 256]), w,
                        ev=ev_ldu[sl])

            def sp_dload(t, di):
                gg = t * 4 + di
                sl = gg % 4
                cg, hf = di // 2, di % 2
                w = [wrest]
                if gg >= 4:
                    w.append(C_["dg_last"][gg - 4])
                C_["dld"][gg] = SP.add(
                    lambda e: e.dma_start(out=Wd[sl][:], in_=Wdv[:, hf * 11:(hf + 1) * 11, cg * 512:(cg + 1) * 512]), w,
                    ev=ev_ldd[sl])

            def next_bank():
                n = C_["unit_cnt"]
                C_["unit_cnt"] += 1
                return n, n % 5

            def bank_wait(n):
                return C_["rd"][n - 5] if n >= 5 else []

            def st_wout(t):
                sl = t % 2
                for s in range(4):
                    for cg in range(2):
                        n, bk = next_bank()
                        w = [C_["mld"][t], wo_ld] + bank_wait(n)

                        def fn(e, s=s, cg=cg, bk=bk):
                            ins = None
                            for kc in range(8):
                                ins = e.matmul(pall[:, bk, :], lhsT=MTt[0][:, kc, s * 128:(s + 1) * 128],
                                               rhs=Wo[:, kc, cg * 512:(cg + 1) * 512], start=(kc == 0), stop=(kc == 7))
                            return ins
                        mm = PE.add(fn, w)
                        C_["wo_done"][t] = mm
                        d = DVE.add(lambda e, s=s, cg=cg, bk=bk: e.tensor_tensor(
                            out=x1[sl][:, s, cg * 512:(cg + 1) * 512], in0=pall[:, bk, :],
                            in1=x1[sl][:, s, cg * 512:(cg + 1) * 512], op=ALU.add), [mm, C_["xld"][t]])
                        C_["rd"][n] = [d]
                        C_["x1add"][t, s] = d

            def st_norm(t):
                sl = t % 2
                aop = None
                for s in range(4):
                    aop = ACT.add(lambda e, s=s: e.activation(out=junk[:], in_=x1[sl][:, s, :], func=AF.Square,
                                                              accum_out=ss4[:, s:s + 1]), [C_["x1add"][t, s]])
                dv = DVE.add(lambda e: e.tensor_scalar(out=var4[:], in0=ss4[:], scalar1=1.0 / D, scalar2=EPS,
                                                       op0=ALU.mult, op1=ALU.add), [aop])
                pw = POOL.add(lambda e: e.tensor_tensor(out=rstd4[:], in0=var4[:], in1=neghalf[:], op=ALU.pow), [dv, mnh])
                hbo = []
                C_["hbo", t] = hbo
                for s in range(4):
                    w = [pw] + c_list
                    if t >= 1:
                        w.append(C_["pe_trh"][t - 1, 3])
                    hbo.append(DVE.add(lambda e, s=s: e.scalar_tensor_tensor(
                        out=hb[:, s, :], in0=x1[sl][:, s, :], scalar=rstd4[:, s:s + 1], in1=g2bc[:],
                        op0=ALU.mult, op1=ALU.mult), w))

            def st_trans(t):
                sl = t % 2
                for s in range(4):
                    bkt = 5 + (s % 2)
                    w = [C_["hbo", t][s], p_id]
                    if len(C_["trh_evac"]) >= 2:
                        w.append(C_["trh_evac"][-2])

                    def fn(e, s=s, bkt=bkt):
                        ins = None
                        for kc in range(8):
                            ins = e.transpose(out=bank_bf(bkt)[:, kc * 128:(kc + 1) * 128],
                                              in_=hb[:, s, kc * 128:(kc + 1) * 128], identity=ident[:])
                        return ins
                    pt = PE.add(fn, w)
                    C_["pe_trh"][t, s] = pt
                    w = [pt]
                    if s == 0 and t >= 1:
                        w.append(C_["last_read_hT"][t - 1])
                    ev = ACT.add(lambda e, s=s, bkt=bkt: e.copy(out=hT[:, :, s * 128:(s + 1) * 128],
                                                                 in_=bank_bf(bkt).rearrange("p (k t) -> p k t", k=8)), w)
                    C_["trh_evac"].append(ev)
                C_["hT_ready"][t] = C_["trh_evac"][-1]

            def st_up(t):
                sl = t % 2
                hT_ready = C_["hT_ready"][t]
                for gi in range(11):
                    gg = t * 11 + gi
                    usl = gg % 3
                    for cl in range(2):
                        c = gi * 2 + cl
                        pc = C_["pair_cnt"]
                        C_["pair_cnt"] += 1
                        wsl = pc % NY
                        res = {}
                        for ab in range(2):
                            ch = c + 22 * ab
                            n, bk = next_bank()
                            w = [C_["uld"][gg], hT_ready] + bank_wait(n)

                            def fn(e, ab=ab, cl=cl, bk=bk, usl=usl):
                                ins = None
                                for kc in range(8):
                                    ins = e.matmul(pall[:, bk, :], lhsT=Wu[usl][:, kc, ab, cl * 128:(cl + 1) * 128],
                                                   rhs=hT[:, kc, :], start=(kc == 0), stop=(kc == 7))
                                return ins
                            mm = PE.add(fn, w)
                            C_["ug_last"][gg] = mm
                            C_["last_read_hT"][t] = mm
                            un = C_["ucnt"]
                            C_["ucnt"] += 1
                            ui = un % NU
                            w = [C_["halo_rd"][ch]]
                            if C_["ub_free"][ui] is not None:
                                w.append(C_["ub_free"][ui])
                            ACT.add(lambda e, ui=ui, ch=ch: e.copy(out=ub[ui][:, 0:2], in_=halo[:, ch, :]), w, sig=False)
                            ev1 = ACT.add(lambda e, ui=ui, bk=bk: e.copy(out=ub[ui][:, 2:514], in_=pall[:, bk, :]), [mm])
                            ydst = (ya if ab == 0 else yb)[wsl]
                            yfree = (C_["ya_free"] if ab == 0 else C_["yb_free"])[wsl]
                            w = c_list[:]
                            if yfree is not None:
                                w.append(yfree)
                            tap0 = ACT.add(lambda e, bk=bk, ch=ch, ydst=ydst: e.activation(
                                out=ydst[:], in_=pall[:, bk, :], func=AF.Identity, scale=fcw[:, 2, ch:ch + 1],
                                bias=fcb[:, ch:ch + 1]), w)
                            hs = ACT.add(lambda e, bk=bk, ch=ch: e.copy(out=halo[:, ch, :], in_=pall[:, bk, 510:512]))
                            C_["rd"][n] = [hs]
                            C_["halo_rd"][ch] = hs
                            DVE.add(lambda e, ui=ui, ch=ch, ydst=ydst: e.scalar_tensor_tensor(
                                out=ydst[:], in0=ub[ui][:, 1:513], scalar=fcw[:, 1, ch:ch + 1], in1=ydst[:],
                                op0=ALU.mult, op1=ALU.add), [tap0, ev1], sig=False)
                            cv = DVE.add(lambda e, ui=ui, ch=ch, ydst=ydst: e.scalar_tensor_tensor(
                                out=ydst[:], in0=ub[ui][:, 0:512], scalar=fcw[:, 0, ch:ch + 1], in1=ydst[:],
                                op0=ALU.mult, op1=ALU.add))
                            C_["ub_free"][ui] = cv
                            res[ab] = cv
                        def fin_pair(t=t, c=c, wsl=wsl, res=res):
                            w = [res[0]]
                            if C_["sa_free"][wsl] is not None:
                                w.append(C_["sa_free"][wsl])
                            si = ACT.add(lambda e: e.activation(out=sa[wsl][:], in_=ya[wsl][:], func=AF.Silu), w)
                            C_["ya_free"][wsl] = si
                            w = [si, res[1]]
                            if t >= 1:
                                w.append(C_["down_last"][t - 1])
                            pr = DVE.add(lambda e: e.tensor_tensor(out=actT[:, c, :], in0=sa[wsl][:], in1=yb[wsl][:],
                                                                   op=ALU.mult), w)
                            C_["sa_free"][wsl] = pr
                            C_["yb_free"][wsl] = pr
                            C_["act_ready"][t, c] = pr
                        if C_["lag"] is not None:
                            C_["lag"]()
                        C_["lag"] = fin_pair
                    nxt = gg + 3
                    if nxt < NTl * 11:
                        sp_uload(nxt // 11, nxt % 11)
                    if gi >= 3 and C_.get("dl_pending"):
                        nd = C_["dl_pending"].pop(0)
                        sp_dload(nd // 4, nd % 4)
                C_["lag"]()
                C_["lag"] = None
                while C_.get("dl_pending"):
                    nd = C_["dl_pending"].pop(0)
                    sp_dload(nd // 4, nd % 4)

            def st_down(t, cg):
                sl = t % 2
                if True:
                    for s in range(4):
                        n, bk = next_bank()
                        w = bank_wait(n) + [C_["act_ready"][t, 21], C_["dld"][t * 4 + cg * 2], C_["dld"][t * 4 + cg * 2 + 1]]

                        def fn(e, s=s, cg=cg, bk=bk):
                            ins = None
                            for c in range(22):
                                dsl = (t * 4 + cg * 2 + c // 11) % 4
                                ins = e.matmul(pall[:, bk, :], lhsT=actT[:, c, s * 128:(s + 1) * 128],
                                               rhs=Wd[dsl][:, c % 11, :], start=(c == 0), stop=(c == 21))
                            return ins
                        mm = PE.add(fn, w)
                        C_["down_last"][t] = mm
                        C_["dg_last"][t * 4 + cg * 2] = mm
                        C_["dg_last"][t * 4 + cg * 2 + 1] = mm
                        d = DVE.add(lambda e, s=s, cg=cg, bk=bk: e.tensor_tensor(
                            out=x1[sl][:, s, cg * 512:(cg + 1) * 512], in0=pall[:, bk, :],
                            in1=x1[sl][:, s, cg * 512:(cg + 1) * 512], op=ALU.add), [mm])
                        C_["rd"][n] = [d]
                        C_["fin_add"][t] = d
                    for k in range(2):
                        nxt = t * 4 + cg * 2 + k + 4
                        if nxt < NTl * 4 and nxt not in C_["dld"]:
                            C_.setdefault("dl_pending", []).append(nxt)

            def st_store(t):
                sl = t % 2
                C_["sto"][sl] = SP.add(lambda e: e.dma_start(out=o_v[t], in_=x1[sl][:]), [C_["fin_add"][t]], ev=ev_sto[sl])
                if t + 2 < NTl:
                    sp_xload(t + 2)


            sp_xload(0)
            if NTl > 1:
                sp_xload(1)
            sp_mload(0)
            for gi in range(3):
                sp_uload(0, gi)
            for di in range(4):
                sp_dload(0, di)
            st_wout(0)
            if NTl > 1:
                sp_mload(1)
            st_norm(0)
            st_trans(0)
            for t in range(NTl):
                st_up(t)
                if t + 1 < NTl:
                    st_wout(t + 1)
                    if t + 2 < NTl:
                        sp_mload(t + 2)
                    st_norm(t + 1)
                st_down(t, 0)
                if t + 1 < NTl:
                    st_trans(t + 1)
                st_down(t, 1)
                st_store(t)
            SP.add(lambda e: e.nop(), [o for o in C_["sto"] if o is not None], ev=None)
            run_block(nc, st)

    if "A" in phases:
        phase_A()
    if "B" in phases:
        phase_B()
    if "C" in phases:
        phase_C()
    es.close()
    return nc


def host_constants(S):
    nch = S // 128
    inv = (np.float32(10000.0) ** (-(np.arange(0, 64, 2, dtype=np.float32)) / np.float32(64))).astype(np.float32)
    ang = (np.arange(S, dtype=np.float32)[:, None] * inv[None, :]).astype(np.float32)
    cos = np.cos(ang).astype(np.float32)
    sin = np.sin(ang).astype(np.float32)
    cosT = np.ascontiguousarray(cos.reshape(nch, 128, 32).transpose(1, 0, 2))
    sinT = np.ascontiguousarray(sin.reshape(nch, 128, 32).transpose(1, 0, 2))
    ident = np.eye(128, dtype=np.float32)
    mneg = np.where(np.arange(128)[:, None] > np.arange(128)[None, :], np.float32(-30000.0), np.float32(0.0)).astype(np.float32)
    return dict(cosT=cosT, sinT=sinT, ident=ident, mneg=mneg)


def bc(v, n=128):
    return np.ascontiguousarray(np.broadcast_to(np.asarray(v, dtype=np.float32).reshape(1, -1), (n, v.size)))


def make_in_maps(inputs, S):
    f = lambda a: np.ascontiguousarray(np.asarray(a, dtype=np.float32))
    cst = host_constants(S)
    shared = dict(
        w_in=f(inputs["w_in"][0]), w_out=f(inputs["w_out"][0]), w_up=f(inputs["w_up"][0]), w_dn=f(inputs["w_down"][0]),
        g1bc=bc(f(inputs["attn_norm_g"][0])), g2bc=bc(f(inputs["ffn_norm_g"][0])),
        qg=bc(f(inputs["q_norm_g"][0])), kg=bc(f(inputs["k_norm_g"][0])),
        lam4=np.ascontiguousarray(np.stack([bc(f(inputs[k][0])) for k in ("lambda_q1", "lambda_k1", "lambda_q2", "lambda_k2")],
                                           axis=1)),
        subg=bc(f(inputs["subln_g"][0])),
        bg=np.ascontiguousarray(f(inputs["b_gate"][0]).reshape(16, 128).T),
        scw=np.ascontiguousarray(f(inputs["short_conv_w"][0]).reshape(3, 8, 128).transpose(2, 0, 1)),
        fcw=np.ascontiguousarray(f(inputs["ffn_conv_w"][0]).reshape(3, 44, 128).transpose(2, 0, 1)),
        fcb=np.ascontiguousarray(f(inputs["ffn_conv_b"][0]).reshape(44, 128).T),
        **cst,
    )
    xs = f(inputs["x"])
    maps = []
    for b in range(xs.shape[0]):
        m = dict(shared)
        m["x"] = np.ascontiguousarray(xs[b])
        maps.append(m)
    return maps


_PROG_CACHE = {}


def kernel(**inputs):
    x = np.asarray(inputs["x"])
    B, S, _ = x.shape
    assert B == NCORES
    if S not in _PROG_CACHE:
        _PROG_CACHE[S] = build_program(S)
    nc = _PROG_CACHE[S]
    in_maps = make_in_maps(inputs, S)
    res = run_bass_kernel_spmd(nc, in_maps, core_ids=list(range(NCORES)))
    return np.stack([np.asarray(r["out"], dtype=np.float32) for r in res.results], axis=0)
```
